# Optimizing a Trainium2 kernel written in Bass

```python
import math
import jax, jax.numpy as jnp
from jax import lax
import numpy as np

D_MODEL = 2048
BATCH = 4
SEQ = 2048
DEPTH = 1
DEC_BATCH = 128
DEC_SEQ = 1
PAST_LEN = 16384
PAGE_SIZE = 128

D_MIX = D_MODEL
D_A = D_MIX // 2
D_B = D_MIX - D_A
CHUNK = 128
HEAD_A = 128
N_HEADS_A = D_A // HEAD_A
GROUP_B = 16
N_GROUPS_B = D_B // GROUP_B
P_STATE = 64
D_IN = 3 * D_A + 2 * D_B
EPS = 1e-6
DT_MIN = 1e-3
DT_MAX = 1e-1

kernel_name = "hymba_gmlp_s5_decode_step"


def rmsnorm(x, g):
    xf = x.astype(jnp.float32)
    r = xf * lax.rsqrt(jnp.mean(xf * xf, axis=-1, keepdims=True) + EPS)
    return (r * g.astype(jnp.float32)).astype(x.dtype)


def layernorm(x, g, b):
    xf = x.astype(jnp.float32)
    mu = jnp.mean(xf, axis=-1, keepdims=True)
    xc = xf - mu
    r = xc * lax.rsqrt(jnp.mean(xc * xc, axis=-1, keepdims=True) + EPS)
    return (r * g.astype(jnp.float32) + b.astype(jnp.float32)).astype(x.dtype)


def adaln(c, w_c, b_c):
    m = jax.nn.silu(c) @ w_c + b_c
    return jnp.split(m, 3, axis=-1)


def chunk_mix(v, w_s, b_s):
    n, L, H, dh = v.shape
    n_chunks = -(-L // CHUNK)
    pad = n_chunks * CHUNK - L
    vp = jnp.pad(v, ((0, 0), (0, pad), (0, 0), (0, 0)))
    vc = vp.reshape(n, n_chunks, CHUNK, H, dh)
    ws = jnp.tril(w_s)
    out = jnp.einsum('hts,bcshd->bcthd', ws, vc) + b_s.T[None, None, :, :, None]
    return out.reshape(n, n_chunks * CHUNK, H, dh)[:, :L]


def s5_discretize(a_re, a_im, log_dt, b_re, b_im):
    a_re = a_re.astype(jnp.float32)
    a_im = a_im.astype(jnp.float32)
    dt = jnp.exp(log_dt.astype(jnp.float32))[:, None]
    mag = jnp.exp(dt * a_re)
    abar_re = mag * jnp.cos(dt * a_im)
    abar_im = mag * jnp.sin(dt * a_im)
    num_re = abar_re - 1.0
    num_im = abar_im
    den = a_re * a_re + a_im * a_im
    coef_re = (num_re * a_re + num_im * a_im) / den
    coef_im = (num_im * a_re - num_re * a_im) / den
    b_re = b_re.astype(jnp.float32)
    b_im = b_im.astype(jnp.float32)
    bbar_re = coef_re[..., None] * b_re - coef_im[..., None] * b_im
    bbar_im = coef_re[..., None] * b_im + coef_im[..., None] * b_re
    return abar_re, abar_im, bbar_re, bbar_im


def _scan_combine(e1, e2):
    a1r, a1i, b1r, b1i = e1
    a2r, a2i, b2r, b2i = e2
    ar = a2r * a1r - a2i * a1i
    ai = a2r * a1i + a2i * a1r
    br = a2r * b1r - a2i * b1i + b2r
    bi = a2r * b1i + a2i * b1r + b2i
    return ar, ai, br, bi


def s5_branch(xb, h0_re, h0_im, a_re, a_im, log_dt, b_re, b_im, c_re, c_im, d_skip, w_glu, b_glu):
    abar_re, abar_im, bbar_re, bbar_im = s5_discretize(a_re, a_im, log_dt, b_re, b_im)
    bu_re = jnp.einsum('nlgc,gpc->nlgp', xb, bbar_re)
    bu_im = jnp.einsum('nlgc,gpc->nlgp', xb, bbar_im)
    bu_re = bu_re.at[:, 0].add(abar_re * h0_re - abar_im * h0_im)
    bu_im = bu_im.at[:, 0].add(abar_re * h0_im + abar_im * h0_re)
    ar = jnp.broadcast_to(abar_re, bu_re.shape)
    ai = jnp.broadcast_to(abar_im, bu_re.shape)
    _, _, h_re, h_im = lax.associative_scan(_scan_combine, (ar, ai, bu_re, bu_im), axis=1)
    d = d_skip.astype(jnp.float32).reshape(N_GROUPS_B, GROUP_B)
    y = (jnp.einsum('nlgp,gcp->nlgc', h_re, c_re.astype(jnp.float32))
         - jnp.einsum('nlgp,gcp->nlgc', h_im, c_im.astype(jnp.float32))
         + d * xb)
    g = jnp.einsum('nlgc,gce->nlge', y, w_glu.astype(jnp.float32)) + b_glu.astype(jnp.float32)
    y = g[..., :GROUP_B] * jax.nn.sigmoid(g[..., GROUP_B:])
    return y, h_re[:, -1], h_im[:, -1]


def hybrid_layer(x, c, h0_re, h0_im, w_c, b_c, g_pre, w_in, ln_v_g, ln_v_b, w_s, b_s,
                 a_re, a_im, log_dt, b_re, b_im, c_re, c_im, d_skip, w_glu, b_glu, w_out, g_post):
    n, L, _ = x.shape
    shift, scale, gate = adaln(c, w_c, b_c)
    h = rmsnorm(x, g_pre) * (1.0 + scale[:, None, :]) + shift[:, None, :]
    proj = h @ w_in
    u_a, v_a, z_a, x_b, z_b = jnp.split(
        proj, [D_A, 2 * D_A, 3 * D_A, 3 * D_A + D_B], axis=-1)
    v_a = layernorm(v_a, ln_v_g, ln_v_b)
    mix = chunk_mix(v_a.reshape(n, L, N_HEADS_A, HEAD_A), w_s, b_s).reshape(n, L, D_A)
    y_a = u_a * mix * jax.nn.silu(z_a)
    xb = x_b.reshape(n, L, N_GROUPS_B, GROUP_B).astype(jnp.float32)
    y_b, h_re, h_im = s5_branch(xb, h0_re, h0_im, a_re, a_im, log_dt, b_re, b_im,
                                c_re, c_im, d_skip, w_glu, b_glu)
    y_b = y_b.reshape(n, L, D_B).astype(x.dtype) * jax.nn.silu(z_b)
    o = jnp.concatenate([y_a, y_b], axis=-1) @ w_out
    out = x + gate[:, None, :] * rmsnorm(o, g_post)
    return out, v_a, h_re, h_im


def setup_inputs(seed: int = 0) -> dict:
    key = jax.random.key(seed)
    ks = jax.random.split(key, 32)
    f32 = jnp.float32
    nrm = lambda k, s: jax.random.normal(k, s, f32)
    x_prompt = nrm(ks[0], (BATCH, SEQ, D_MODEL))
    x_sample = nrm(ks[1], (DEC_BATCH, DEC_SEQ, D_MODEL))
    c_prompt = nrm(ks[2], (BATCH, D_MODEL))
    c_sample = nrm(ks[3], (DEC_BATCH, D_MODEL))
    state_b_re = 0.3 * nrm(ks[4], (DEC_BATCH, N_GROUPS_B, P_STATE))
    state_b_im = 0.3 * nrm(ks[5], (DEC_BATCH, N_GROUPS_B, P_STATE))
    w_c = 0.2 * nrm(ks[6], (D_MODEL, 3 * D_MODEL)) * D_MODEL ** -0.5
    b_c = 0.02 * nrm(ks[7], (3 * D_MODEL,))
    g_pre = 1.0 + 0.05 * nrm(ks[8], (D_MODEL,))
    w_in = nrm(ks[9], (D_MODEL, D_IN)) * D_MODEL ** -0.5
    ln_v_g = 1.0 + 0.05 * nrm(ks[10], (D_A,))
    ln_v_b = 0.02 * nrm(ks[11], (D_A,))
    w_s = nrm(ks[12], (N_HEADS_A, CHUNK, CHUNK)) * CHUNK ** -0.5
    b_s = 1.0 + 0.1 * nrm(ks[13], (N_HEADS_A, CHUNK))
    a_re = -0.5 + 0.01 * nrm(ks[14], (N_GROUPS_B, P_STATE))
    a_im = math.pi * jnp.arange(P_STATE, dtype=f32)[None, :] + 0.01 * nrm(ks[15], (N_GROUPS_B, P_STATE))
    log_dt = jax.random.uniform(ks[16], (N_GROUPS_B,), f32, math.log(DT_MIN), math.log(DT_MAX))
    b_scale = (2.0 * GROUP_B) ** -0.5
    b_re = b_scale * nrm(ks[17], (N_GROUPS_B, P_STATE, GROUP_B))
    b_im = b_scale * nrm(ks[18], (N_GROUPS_B, P_STATE, GROUP_B))
    c_scale = (2.0 * P_STATE) ** -0.5
    c_re = c_scale * nrm(ks[19], (N_GROUPS_B, GROUP_B, P_STATE))
    c_im = c_scale * nrm(ks[20], (N_GROUPS_B, GROUP_B, P_STATE))
    d_skip = nrm(ks[21], (D_B,))
    w_glu = nrm(ks[22], (N_GROUPS_B, GROUP_B, 2 * GROUP_B)) * GROUP_B ** -0.5
    b_glu = 0.02 * nrm(ks[23], (N_GROUPS_B, 2 * GROUP_B))
    w_out = nrm(ks[24], (D_MIX, D_MODEL)) * D_MIX ** -0.5
    g_post = 1.0 + 0.05 * nrm(ks[25], (D_MODEL,))
    return {"x_prompt": x_prompt, "x_sample": x_sample, "c_prompt": c_prompt, "c_sample": c_sample,
            "state_b_re": state_b_re, "state_b_im": state_b_im,
            "w_c": w_c, "b_c": b_c, "g_pre": g_pre, "w_in": w_in, "ln_v_g": ln_v_g, "ln_v_b": ln_v_b,
            "w_s": w_s, "b_s": b_s, "a_re": a_re, "a_im": a_im, "log_dt": log_dt,
            "b_re": b_re, "b_im": b_im, "c_re": c_re, "c_im": c_im, "d_skip": d_skip,
            "w_glu": w_glu, "b_glu": b_glu, "w_out": w_out, "g_post": g_post}


def reference(x_prompt, x_sample, c_prompt, c_sample, state_b_re, state_b_im,
              w_c, b_c, g_pre, w_in, ln_v_g, ln_v_b, w_s, b_s, a_re, a_im, log_dt,
              b_re, b_im, c_re, c_im, d_skip, w_glu, b_glu, w_out, g_post):
    y_prompt = x_prompt
    y_sample = x_sample
    for layer in range(DEPTH):
        h0_re_p = jnp.zeros((x_prompt.shape[0], N_GROUPS_B, P_STATE), jnp.float32)
        h0_im_p = jnp.zeros((x_prompt.shape[0], N_GROUPS_B, P_STATE), jnp.float32)
        y_prompt, _, hp_re, hp_im = hybrid_layer(
            y_prompt, c_prompt, h0_re_p, h0_im_p, w_c, b_c, g_pre, w_in, ln_v_g, ln_v_b, w_s, b_s,
            a_re, a_im, log_dt, b_re, b_im, c_re, c_im, d_skip, w_glu, b_glu, w_out, g_post)
        y_sample, v_s, hs_re, hs_im = hybrid_layer(
            y_sample, c_sample, state_b_re.astype(jnp.float32), state_b_im.astype(jnp.float32),
            w_c, b_c, g_pre, w_in, ln_v_g, ln_v_b, w_s, b_s,
            a_re, a_im, log_dt, b_re, b_im, c_re, c_im, d_skip, w_glu, b_glu, w_out, g_post)
    return (y_prompt, y_sample, v_s, hp_re, hp_im, hs_re, hs_im)
```

```python
import math
from contextlib import ExitStack

import numpy as np
import concourse.bass as bass
import concourse.mybir as mybir
from concourse.bass_utils import run_bass_kernel_spmd

F32 = mybir.dt.float32
BF16 = mybir.dt.bfloat16
I32 = mybir.dt.int32
ALU = mybir.AluOpType
AF = mybir.ActivationFunctionType

D = 2048
D_A = 1024
D_IN = 5120
EPS = 1e-6
TWO_PI = 2.0 * math.pi
NCORES = 8
PIPELINE = True


class Sched:
    ENG = ("pe", "act", "dve", "pool", "sp")

    def __init__(self, sems, dma_sems):
        self.streams = {e: [] for e in self.ENG}
        self.sem = sems
        self.tick = {e: 0 for e in self.ENG}
        self.waited = {e: {} for e in self.ENG}
        self.last_w = {}
        self.readers = {}
        self.dma_sems = dma_sems
        self.dma_i = {q: 0 for q in dma_sems}
        self.out_events = []

    def _need(self, eng, ev):
        if ev is None:
            return
        sem, val, src = ev
        if src == eng and eng == "pe":
            return
        key = id(sem)
        if self.waited[eng].get(key, 0) >= val:
            return
        self.waited[eng][key] = val
        self.streams[eng].append(("wait", sem, val))

    def _deps(self, eng, reads, writes):
        for k in reads:
            self._need(eng, self.last_w.get(k))
        for k in writes:
            self._need(eng, self.last_w.get(k))
            for ev in self.readers.get(k, []):
                self._need(eng, ev)

    def _commit(self, ev, reads, writes):
        for k in reads:
            self.readers.setdefault(k, []).append(ev)
        for k in writes:
            self.last_w[k] = ev
            self.readers[k] = []

    def op(self, eng, fn, reads=(), writes=()):
        self._deps(eng, reads, writes)
        self.tick[eng] += 1
        ev = (self.sem[eng], self.tick[eng], eng)
        self.streams[eng].append(("op", fn, self.sem[eng], 1))
        self._commit(ev, reads, writes)
        return ev

    def pe(self, fns, reads=(), writes=()):
        self._deps("pe", reads, writes)
        for fn in fns[:-1]:
            self.streams["pe"].append(("op", fn, None, 0))
        self.tick["pe"] += 1
        ev = (self.sem["pe"], self.tick["pe"], "pe")
        self.streams["pe"].append(("op", fns[-1], self.sem["pe"], 1))
        self._commit(ev, reads, writes)
        return ev

    def dma(self, q, fn, reads=(), writes=(), is_out=False):
        ring = self.dma_sems[q]
        i = self.dma_i[q]
        self.dma_i[q] += 1
        sem = ring[i % len(ring)]
        gen = i // len(ring)
        if gen > 0:
            self._need(q, (sem, 16 * gen, "dma"))
        self._deps(q, reads, writes)
        ev = (sem, 16 * (gen + 1), "dma")
        self.streams[q].append(("op", fn, sem, 16))
        self._commit(ev, reads, writes)
        if is_out:
            self.out_events.append(ev)
        return ev

    def barrier_all(self):
        engs = ("pe", "act", "dve", "pool", "sp")
        evs = [(self.sem[e], self.tick[e], e) for e in ("pe", "act", "dve", "pool") if self.tick[e] > 0]
        for q, ring in self.dma_sems.items():
            n = self.dma_i[q]
            for j, sem in enumerate(ring):
                cnt = (n - j + len(ring) - 1) // len(ring) if n > j else 0
                if cnt > 0:
                    evs.append((sem, 16 * cnt, "dma"))
        for e in engs:
            for ev in evs:
                if not (ev[2] == e and e == "pe"):
                    self._need(e, ev)
        self.last_w = {}
        self.readers = {}

    def finish(self):
        for ev in self.out_events:
            self._need("sp", ev)

    def emit(self, block):
        def run(engine, name):
            for item in self.streams[name]:
                if item[0] == "wait":
                    engine.wait_ge(item[1], item[2])
                else:
                    _, fn, sem, inc = item
                    ins = fn(engine)
                    if sem is not None:
                        ins.then_inc(sem, inc)

        @block.tensor
        def _(e):
            run(e, "pe")

        @block.scalar
        def _(e):
            run(e, "act")

        @block.vector
        def _(e):
            run(e, "dve")

        @block.gpsimd
        def _(e):
            run(e, "pool")

        @block.sync
        def _(e):
            run(e, "sp")


class Arena:
    def __init__(self, ar, nbytes):
        self.ar = ar
        self.n = nbytes
        self.off = 0

    def alloc(self, nbytes):
        o = self.off
        self.off += (nbytes + 63) // 64 * 64
        assert self.off <= self.n, f"arena overflow {self.off} > {self.n}"
        return o

    def view(self, off, shape, dt, parts=128, p0=0):
        esz = 2 if dt == BF16 else 4
        nelem = int(np.prod(shape))
        w0 = off // 4
        nw = nelem * esz // 4
        a = self.ar[p0:p0 + parts, w0:w0 + nw]
        if dt != F32:
            a = a.bitcast(dt)
        if len(shape) == 2:
            a = a.rearrange("p (a b) -> p a b", a=shape[0])
        elif len(shape) == 3:
            a = a.rearrange("p (a b c) -> p a b c", a=shape[0], b=shape[1])
        return a

    def new(self, shape, dt, parts=128):
        esz = 2 if dt == BF16 else 4
        off = self.alloc(int(np.prod(shape)) * esz)
        return self.view(off, shape, dt, parts), off


def build_nc():
    nc = bass.Bass("TRN2", target_bir_lowering=False)

    def din(name, shape):
        return nc.dram_tensor(name, list(shape), F32, kind="ExternalInput").ap()

    def dout(name, shape):
        return nc.dram_tensor(name, list(shape), F32, kind="ExternalOutput").ap()

    xo = din("xo", (1024, D)); xp = din("xp", (1024, D)); xs = din("xs", (16, D))
    cs_d = din("cs", (33, D)); sre_d = din("sre", (16, 4096)); sim_d = din("sim", (16, 4096))
    maskv_d = din("maskv", (128, 1))
    w_c = din("w_c", (D, 3 * D)); b_c = din("b_c", (1, 3 * D))
    gpre_d = din("gpre_fm", (128, 16)); gpost_d = din("gpost_fm", (128, 16))
    w_in = din("w_in", (D, D_IN)); w_out = din("w_out", (D, D))
    lng_d = din("lng", (1, 1024)); lnb_d = din("lnb", (1, 1024))
    wst_d = din("ws_t", (128, 1024)); bsrow_d = din("bsrow", (1, 1024))
    w00_d = din("w00", (1, 8)); bs0_d = din("bs0", (1, 8))
    are_d = din("a_re2", (128, 32)); aim_d = din("a_im2", (128, 32)); ldt_d = din("ldt2", (128, 32))
    Bre_d = din("Bre_pad", (128, 4096)); Bim_d = din("Bim_pad", (128, 4096))
    Cre_d = din("Cre_pad", (128, 4096)); Cim_d = din("Cim_pad", (128, 4096))
    dfm_d = din("dfm", (128, 8)); wgv_d = din("wgv_pad", (128, 1024)); wgg_d = din("wgg_pad", (128, 1024))
    bgv_d = din("bgv", (128, 8)); bgg_d = din("bgg", (128, 8))

    yo = dout("yo", (1024, D)); ys = dout("ys", (16, D)); vs_o = dout("vs", (16, 1024))
    hpre_o = dout("hp_re", (32, 128)); hpim_o = dout("hp_im", (32, 128))
    hsre_o = dout("hs_re", (16, 4096)); hsim_o = dout("hs_im", (16, 4096))

    w_in_v = w_in.rearrange("(kc p) n -> p kc n", p=128)
    w_out_v = w_out.rearrange("(kc p) n -> p kc n", p=128)
    w_c_v = w_c.rearrange("(kc p) n -> p kc n", p=128)

    ARENA_BYTES = 208896
    with ExitStack() as es:
        ar_t = es.enter_context(nc.sbuf_tensor("arena", [128, ARENA_BYTES // 4], F32))
        A = Arena(ar_t, ARENA_BYTES)

        def ps(name, shape, dt):
            return es.enter_context(nc.psum_tensor(name, shape, dt))

        def sem(name):
            return es.enter_context(nc.semaphore(name))

        pm = [ps("pm0", [128, 512], F32), ps("pm1", [128, 512], F32)]
        pbr = ps("pbr", [128, 4, 128], F32)
        pbi = ps("pbi", [128, 4, 128], F32)
        pyg = ps("pyg", [128, 8, 128], F32)
        ptb = ps("ptb", [128, 4, 128], BF16)
        px = ps("px", [128, 512], F32)
        pssq = pbr[:].rearrange("p a b -> p (a b)")

        sems = {e: sem("s_" + e) for e in Sched.ENG}
        dsems = {q: [sem(f"d_{q}{i}") for i in range(8)] for q in ("sp", "pool", "act")}
        block = es.enter_context(nc.Block())
        S = Sched(sems, dsems)

        hT, hT_off = A.new((16, 512), BF16)
        yT, yT_off = A.new((16, 512), BF16)
        vraw, vraw_off = A.new((4, 1024), F32)
        Ct, Ct_off = A.new((32, 128), F32)
        St, St_off = A.new((32, 128), F32)
        Bre, _ = A.new((32, 128), BF16)
        Bim, _ = A.new((32, 128), BF16)
        Cre, Cre_off = A.new((32, 128), BF16)
        nCim, nCim_off = A.new((32, 128), BF16)
        Wgv, _ = A.new((8, 128), BF16)
        Wgg, _ = A.new((8, 128), BF16)
        NWB = 3
        wblk_alloc = [A.new((16, 128), BF16) for _ in range(NWB)]
        wblk = [v for v, _ in wblk_alloc]
        wblk_off = wblk_alloc[0][1]
        xbT, _ = A.new((512,), BF16)
        szb, _ = A.new((512,), F32)
        u_sb, u_off = A.new((512,), F32)
        sza, sza_off = A.new((512,), F32)
        xbT2 = [xbT, A.view(sza_off, (512,), BF16)]
        szb2 = [szb, A.view(u_off, (512,), F32)]
        s5w = [A.new((4, 128), F32) for _ in range(6)]
        tA, tB, tC, tD, gr, gi = [v for v, _ in s5w]
        s5w_off = s5w[0][1]
        hrb2 = [A.new((4, 128), BF16)[0] for _ in range(2)]
        hib2 = [A.new((4, 128), BF16)[0] for _ in range(2)]
        ybf2 = [A.new((128,), BF16)[0] for _ in range(2)]
        sig2 = [A.new((128,), F32)[0] for _ in range(2)]
        hrb, hib = hrb2[0], hib2[0]
        tmp128, _ = A.new((128,), F32)
        xt, xt_off = A.new((D,), F32)
        outt, outt_off = A.new((D,), F32)
        xn = None
        Wt, _ = A.new((8, 128), F32)
        Bsbc, _ = A.new((8, 128), F32)
        lng, _ = A.new((1024,), F32)
        lnb, _ = A.new((1024,), F32)
        Afm, _ = A.new((16, 33), F32)
        Shfm, _ = A.new((16, 33), F32)
        GGfm, _ = A.new((16, 33), F32)
        gpre, _ = A.new((16,), F32)
        gpost, _ = A.new((16,), F32)
        identf, _ = A.new((128,), F32)
        identb, _ = A.new((128,), BF16)
        ones_f, _ = A.new((128,), F32)
        small = {}
        for nm in ("are", "aim", "dt", "rho", "th", "abr", "abi", "cfr", "cfi", "icr", "ici", "e1r", "e1i",
                   "eTr", "eTi", "ginr", "gini", "p1", "p2", "p3", "p4", "p5", "p6"):
            small[nm], _ = A.new((32,), F32)
        small_i, _ = A.new((32,), I32)
        A128r, _ = A.new((32,), F32)
        A128i, _ = A.new((32,), F32)
        gin2r, _ = A.new((8, 16), F32)
        gin2i, _ = A.new((8, 16), F32)
        dfm, _ = A.new((8,), F32); bgv, _ = A.new((8,), F32); bgg, _ = A.new((8,), F32)
        w00bc, _ = A.new((8,), F32); bs0bc, _ = A.new((8,), F32)
        Wsamp, _ = A.new((8, 16), F32)
        maskv, _ = A.new((1,), F32)
        st4, _ = A.new((8,), F32)
        rstd_bc, rstd_off = A.new((512,), F32)
        zt, zt_off = A.new((512,), F32)
        sqt = A.view(zt_off, (512,), F32)
        assert zt_off == rstd_off + 2048
        xn = A.view(rstd_off, (D,), BF16)
        tauf, _ = A.new((128,), F32)
        taui, _ = A.new((128,), I32)
        hTs, _ = A.new((16, 16), BF16)
        yTs, _ = A.new((16, 16), BF16)
        print("arena used", A.off)

        oT_lo = A.view(hT_off, (8, 512), F32)
        oT_hi = A.view(vraw_off, (8, 512), F32)
        oTs = A.view(u_off, (16, 16), F32)
        wblk_samp = wblk + [A.view(hT_off + i * 4096, (16, 128), BF16) for i in range(4)] \
                         + [A.view(yT_off + i * 4096, (16, 128), BF16) for i in range(4)]
        cur = {"yT": yT}
        RCt = A.view(yT_off, (32, 128), F32)
        RSt = A.view(vraw_off, (32, 128), F32)
        accs = A.view(s5w[0][1], (4, 32, 4), F32)
        junkR = A.view(s5w[1][1], (128,), F32)
        tauR = A.view(xt_off, (128,), F32)
        wcbB = [A.view(Cre_off + i * 4096, (4, 512), BF16) for i in range(2)]
        wcbF = [A.view(xt_off + i * 8192, (4, 512), F32) for i in range(2)]
        m3 = A.view(nCim_off, (D,), F32, parts=33)
        cs_t = A.view(wblk_off, (D,), F32, parts=33)
        bcb = A.view(s5w_off, (D,), F32, parts=33)
        csT = A.view(s5w_off + 8192, (16, 33), BF16)
        csTf = A.view(s5w_off + 8192 + 1088, (16, 33), F32)
        targ = A.view(hT_off, (32, 128), F32)
        tq = A.view(yT_off, (32, 128), F32)
        tqi = A.view(vraw_off, (32, 128), I32)
        cstage = A.view(vraw_off, (32, 128), F32)
        cstage2 = A.view(hT_off, (32, 128), F32)
        sre_t = A.view(vraw_off, (4096,), F32, parts=16)
        sim_t = A.view(xt_off, (4096,), F32, parts=16)
        targ_alt = A.view(xt_off, (32, 128), F32)
        rhoZ = A.view(vraw_off, (32, 128), F32)

        defer = {"on": False, "ops": []}

        def dve(fn, r=(), w=()):
            if defer["on"]:
                defer["ops"].append(("dve", fn, r, w))
                return None
            return S.op("dve", fn, r, w)

        def act(fn, r=(), w=()):
            if defer["on"]:
                defer["ops"].append(("act", fn, r, w))
                return None
            return S.op("act", fn, r, w)

        def op_cost(w):
            return 40 if any(k.startswith("targ") or k in ("St", "Ct") for k in w) else 1

        def emit_deferred(budget):
            spent = 0
            while defer["ops"] and spent < budget:
                eng, fn, r, w = defer["ops"].pop(0)
                S.op(eng, fn, r, w)
                spent += op_cost(w)

        def pool(fn, r=(), w=()):
            return S.op("pool", fn, r, w)

        def range_reduce(arg, q, qi, shape_key):
            dve(lambda e: e.tensor_scalar(out=q, in0=arg, scalar1=1.0 / TWO_PI, scalar2=None, op0=ALU.mult),
                [shape_key], [shape_key + "q"])
            dve(lambda e: e.tensor_copy(out=qi, in_=q), [shape_key + "q"], [shape_key + "qi"])
            dve(lambda e: e.tensor_copy(out=q, in_=qi), [shape_key + "qi"], [shape_key + "q"])
            dve(lambda e: e.scalar_tensor_tensor(out=arg, in0=q, scalar=-TWO_PI, in1=arg, op0=ALU.mult, op1=ALU.add),
                [shape_key + "q", shape_key], [shape_key])
            dve(lambda e: e.tensor_scalar(out=q, in0=arg, scalar1=math.pi, scalar2=-TWO_PI, op0=ALU.is_gt, op1=ALU.mult),
                [shape_key], [shape_key + "q"])
            dve(lambda e: e.tensor_tensor(out=arg, in0=arg, in1=q, op=ALU.add), [shape_key, shape_key + "q"], [shape_key])
            dve(lambda e: e.tensor_scalar(out=q, in0=arg, scalar1=-math.pi, scalar2=TWO_PI, op0=ALU.is_lt, op1=ALU.mult),
                [shape_key], [shape_key + "q"])
            dve(lambda e: e.tensor_tensor(out=arg, in0=arg, in1=q, op=ALU.add), [shape_key, shape_key + "q"], [shape_key])
            dve(lambda e: e.tensor_scalar(out=arg, in0=arg, scalar1=math.pi, scalar2=-math.pi, op0=ALU.min, op1=ALU.max),
                [shape_key], [shape_key])

        pool(lambda e: e.memset(identf, 0.0), [], ["identf"])
        pool(lambda e: e.affine_select(out=identf, in_=identf, compare_op=ALU.not_equal, fill=1.0, base=0,
                                       pattern=[[-1, 128]], channel_multiplier=1), ["identf"], ["identf"])
        dve(lambda e: e.tensor_copy(out=identb, in_=identf), ["identf"], ["identb"])
        dve(lambda e: e.memset(ones_f, 1.0), [], ["ones_f"])
        pool(lambda e: e.iota(taui, pattern=[[1, 128]], base=0, channel_multiplier=0), [], ["taui"])
        dve(lambda e: e.tensor_copy(out=tauf, in_=taui), ["taui"], ["tauf"])

        def ld(dst, src, key, q="sp"):
            S.dma(q, lambda e: e.dma_start(out=dst, in_=src), [], [key])

        ld(gpre, gpre_d, "gpre"); ld(gpost, gpost_d, "gpost")
        ld(small["are"], are_d, "are"); ld(small["aim"], aim_d, "aim"); ld(small["dt"], ldt_d, "dt")
        ld(dfm, dfm_d, "dfm"); ld(bgv, bgv_d, "bgv"); ld(bgg, bgg_d, "bgg"); ld(maskv, maskv_d, "maskv")
        ld(w00bc[0:16], w00_d.partition_broadcast(16), "w00bc")
        ld(bs0bc, bs0_d.partition_broadcast(128), "bs0bc")
        ld(lng, lng_d.partition_broadcast(128), "lng"); ld(lnb, lnb_d.partition_broadcast(128), "lnb")
        ld(Bsbc.rearrange("p a b -> p (a b)"), bsrow_d.partition_broadcast(128), "Bsbc")
        ld(Wt.rearrange("p a b -> p (a b)"), wst_d, "Wt")
        ld(cs_t, cs_d, "cs")
        S.dma("pool", lambda e: e.dma_start(out=Bre.rearrange("p a b -> p (a b)"), in_=Bre_d), [], ["Bre"])
        S.dma("pool", lambda e: e.dma_start(out=Bim.rearrange("p a b -> p (a b)"), in_=Bim_d), [], ["Bim"])
        S.dma("pool", lambda e: e.dma_start(out=Wgv.rearrange("p a b -> p (a b)"), in_=wgv_d), [], ["Wgv"])
        S.dma("pool", lambda e: e.dma_start(out=Wgg.rearrange("p a b -> p (a b)"), in_=wgg_d), [], ["Wgg"])
        for h in range(8):
            pool(lambda e, h=h: e.affine_select(out=Wt[:, h, :], in_=Wt[:, h, :], compare_op=ALU.is_ge, fill=0.0,
                                                base=0, pattern=[[1, 128]], channel_multiplier=-1), ["Wt"], ["Wt"])
        for h in range(8):
            dve(lambda e, h=h: e.tensor_scalar(out=Wsamp[0:16, h, :], in0=identf[0:16, 0:16], scalar1=w00bc[0:16, h:h + 1],
                                               scalar2=None, op0=ALU.mult), ["identf", "w00bc"], ["Wsamp"])

        defer["on"] = True
        sm = small
        act(lambda e: e.activation(out=sm["dt"], in_=sm["dt"], func=AF.Exp), ["dt"], ["dt"])
        dve(lambda e: e.tensor_tensor(out=sm["p1"], in0=sm["dt"], in1=sm["are"], op=ALU.mult), ["dt", "are"], ["p1"])
        act(lambda e: e.activation(out=sm["rho"], in_=sm["p1"], func=AF.Exp), ["p1"], ["rho"])
        dve(lambda e: e.tensor_tensor(out=sm["th"], in0=sm["dt"], in1=sm["aim"], op=ALU.mult), ["dt", "aim"], ["th"])

        def sincos(arg_src, mult, out_s, out_c, tagp):
            dve(lambda e: e.tensor_scalar(out=sm["p2"], in0=arg_src, scalar1=float(mult), scalar2=None, op0=ALU.mult),
                ["th"], ["p2"])
            range_reduce(sm["p2"], sm["p3"], small_i, "p2")
            act(lambda e: e.activation(out=out_s, in_=sm["p2"], func=AF.Sin), ["p2"], [tagp + "s"])
            dve(lambda e: e.tensor_scalar(out=sm["p2"], in0=arg_src, scalar1=float(mult), scalar2=math.pi / 2,
                                          op0=ALU.mult, op1=ALU.add), ["th", tagp + "s"], ["p2"])
            range_reduce(sm["p2"], sm["p3"], small_i, "p2")
            act(lambda e: e.activation(out=out_c, in_=sm["p2"], func=AF.Sin), ["p2"], [tagp + "c"])

        sincos(sm["th"], 1.0, sm["e1i"], sm["e1r"], "e1")
        sincos(sm["th"], 128.0, sm["eTi"], sm["eTr"], "eT")
        K1 = ["e1s", "e1c", "rho"]
        dve(lambda e: e.tensor_tensor(out=sm["abr"], in0=sm["rho"], in1=sm["e1r"], op=ALU.mult), K1, ["abr"])
        dve(lambda e: e.tensor_tensor(out=sm["abi"], in0=sm["rho"], in1=sm["e1i"], op=ALU.mult), K1, ["abi"])
        dve(lambda e: e.tensor_scalar(out=sm["p1"], in0=sm["abr"], scalar1=-1.0, scalar2=None, op0=ALU.add), ["abr"], ["p1"])
        dve(lambda e: e.tensor_tensor(out=sm["p4"], in0=sm["are"], in1=sm["are"], op=ALU.mult), ["are"], ["p4"])
        dve(lambda e: e.tensor_tensor(out=sm["p5"], in0=sm["aim"], in1=sm["aim"], op=ALU.mult), ["aim"], ["p5"])
        dve(lambda e: e.tensor_tensor(out=sm["p4"], in0=sm["p4"], in1=sm["p5"], op=ALU.add), ["p4", "p5"], ["p4"])
        dve(lambda e: e.reciprocal(out=sm["p4"], in_=sm["p4"]), ["p4"], ["p4"])
        dve(lambda e: e.tensor_tensor(out=sm["p5"], in0=sm["p1"], in1=sm["are"], op=ALU.mult), ["p1", "are"], ["p5"])
        dve(lambda e: e.tensor_tensor(out=sm["p6"], in0=sm["abi"], in1=sm["aim"], op=ALU.mult), ["abi", "aim"], ["p6"])
        dve(lambda e: e.tensor_tensor(out=sm["p5"], in0=sm["p5"], in1=sm["p6"], op=ALU.add), ["p5", "p6"], ["p5"])
        dve(lambda e: e.tensor_tensor(out=sm["cfr"], in0=sm["p5"], in1=sm["p4"], op=ALU.mult), ["p5", "p4"], ["cfr"])
        dve(lambda e: e.tensor_tensor(out=sm["p5"], in0=sm["abi"], in1=sm["are"], op=ALU.mult), ["abi", "are", "cfr"], ["p5"])
        dve(lambda e: e.tensor_tensor(out=sm["p6"], in0=sm["p1"], in1=sm["aim"], op=ALU.mult), ["p1", "aim"], ["p6"])
        dve(lambda e: e.tensor_tensor(out=sm["p5"], in0=sm["p5"], in1=sm["p6"], op=ALU.subtract), ["p5", "p6"], ["p5"])
        dve(lambda e: e.tensor_tensor(out=sm["cfi"], in0=sm["p5"], in1=sm["p4"], op=ALU.mult), ["p5", "p4"], ["cfi"])
        dve(lambda e: e.tensor_tensor(out=sm["p4"], in0=sm["cfr"], in1=sm["cfr"], op=ALU.mult), ["cfr", "cfi"], ["p4"])
        dve(lambda e: e.tensor_tensor(out=sm["p5"], in0=sm["cfi"], in1=sm["cfi"], op=ALU.mult), ["cfi", "p4"], ["p5"])
        dve(lambda e: e.tensor_tensor(out=sm["p4"], in0=sm["p4"], in1=sm["p5"], op=ALU.add), ["p4", "p5"], ["p4"])
        dve(lambda e: e.reciprocal(out=sm["p4"], in_=sm["p4"]), ["p4"], ["p4"])
        dve(lambda e: e.tensor_tensor(out=sm["icr"], in0=sm["cfr"], in1=sm["p4"], op=ALU.mult), ["cfr", "p4"], ["icr"])
        dve(lambda e: e.scalar_tensor_tensor(out=sm["ici"], in0=sm["cfi"], scalar=-1.0, in1=sm["p4"], op0=ALU.mult, op1=ALU.mult),
            ["cfi", "p4"], ["ici"])
        dve(lambda e: e.memset(gin2r, 0.0), [], ["gin"])
        dve(lambda e: e.memset(gin2i, 0.0), ["gin"], ["gin"])

        th_b = sm["th"].unsqueeze(2).to_broadcast([128, 32, 128])
        tau_b = tauf.unsqueeze(1).to_broadcast([128, 32, 128])
        dve(lambda e: e.tensor_tensor(out=targ, in0=th_b, in1=tau_b, op=ALU.mult), ["th", "tauf"], ["targ"])
        range_reduce(targ, tq, tqi, "targ")
        act(lambda e: e.activation(out=St, in_=targ, func=AF.Sin), ["targ"], ["St"])
        dve(lambda e: e.tensor_tensor(out=targ, in0=th_b, in1=tau_b, op=ALU.mult), ["th", "tauf", "St"], ["targ"])
        dve(lambda e: e.tensor_scalar(out=targ, in0=targ, scalar1=math.pi / 2, scalar2=None, op0=ALU.add), ["targ"], ["targ"])
        range_reduce(targ, tq, tqi, "targ")
        act(lambda e: e.activation(out=Ct, in_=targ, func=AF.Sin), ["targ"], ["Ct"])
        defer["on"] = False
        per_step = (sum(op_cost(o[3]) for o in defer["ops"]) + 10) // 11
        act(lambda e: e.activation(out=cs_t, in_=cs_t, func=AF.Silu), ["cs"], ["cs"])
        for kc in range(16):
            S.pe([lambda e, kc=kc: e.transpose(px[:, 0:33], cs_t[:, kc * 128:(kc + 1) * 128], identf[0:33, 0:33])],
                 ["cs", "identf"], ["px"])
            dve(lambda e, kc=kc: e.tensor_copy(out=csT[:, kc, :], in_=px[:, 0:33]), ["px"], ["csT"])
            dve(lambda e, kc=kc: e.tensor_copy(out=csTf[:, kc, :], in_=px[:, 0:33]), ["px"], ["csT"])
        wci = 0
        for third, dstT in enumerate((Shfm, Afm, GGfm)):
            S.dma("sp", lambda e, third=third: e.dma_start(out=bcb, in_=b_c[:, third * D:(third + 1) * D].partition_broadcast(33)),
                  [], ["bcb"])
            for nb in range(4):
                col0 = third * D + nb * 512
                fns = []
                for kg in range(4):
                    half = wci // 2
                    if kg % 2 == 0:
                        buf = wcbB[half % 2]; key = f"wcbB{half % 2}"; q = "pool"; lsrc = csT
                    else:
                        buf = wcbF[half % 2]; key = f"wcbF{half % 2}"; q = "sp" if half % 2 == 0 else "act"; lsrc = csTf
                    wci += 1
                    S.dma(q,
                          lambda e, buf=buf, kg=kg, col0=col0: e.dma_start(out=buf, in_=w_c_v[:, kg * 4:(kg + 1) * 4, col0:col0 + 512]),
                          [], [key])
                    S.pe([lambda e, buf=buf, kg=kg, i=i, nb=nb, lsrc=lsrc: e.matmul(pm[nb % 2][0:33, :], lhsT=lsrc[:, kg * 4 + i, :], rhs=buf[:, i, :],
                                                                        start=(kg == 0 and i == 0), stop=(kg == 3 and i == 3))
                          for i in range(4)], [key, "csT"], [f"pm{nb % 2}"])
                dve(lambda e, nb=nb: e.tensor_tensor(out=m3[:, nb * 512:(nb + 1) * 512], in0=pm[nb % 2][0:33, :],
                                                     in1=bcb[:, nb * 512:(nb + 1) * 512], op=ALU.add), [f"pm{nb % 2}", "bcb"], ["m3"])
                emit_deferred(per_step)
            for kc in range(16):
                S.pe([lambda e, kc=kc: e.transpose(px[:, 0:33], m3[:, kc * 128:(kc + 1) * 128], identf[0:33, 0:33])],
                     ["m3", "identf"], ["px"])
                dve(lambda e, kc=kc, dstT=dstT: e.tensor_copy(out=dstT[:, kc, :], in_=px[:, 0:33]), ["px"], ["ada"])
        dve(lambda e: e.tensor_scalar(out=Afm, in0=Afm, scalar1=1.0, scalar2=None, op0=ALU.add), ["ada"], ["ada"])
        dve(lambda e: e.tensor_tensor(out=Afm, in0=Afm, in1=gpre.unsqueeze(2).to_broadcast([128, 16, 33]), op=ALU.mult),
            ["ada", "gpre"], ["ada"])
        dve(lambda e: e.tensor_tensor(out=GGfm, in0=GGfm, in1=gpost.unsqueeze(2).to_broadcast([128, 16, 33]), op=ALU.mult),
            ["ada", "gpost"], ["ada"])
        emit_deferred(10 ** 9)
        S.barrier_all()
        S.dma("sp", lambda e: e.dma_start(out=cstage.rearrange("p a b -> p (a b)"), in_=Cre_d), [], ["cst"])
        S.dma("act", lambda e: e.dma_start(out=cstage2.rearrange("p a b -> p (a b)"), in_=Cim_d), [], ["cst2"])
        cf_r = sm["cfr"].unsqueeze(2).to_broadcast([128, 32, 128])
        cf_i = sm["cfi"].unsqueeze(2).to_broadcast([128, 32, 128])
        dve(lambda e: e.tensor_tensor(out=tq, in0=cstage, in1=cf_r, op=ALU.mult), ["cst", "cfr"], ["tq"])
        dve(lambda e: e.tensor_tensor(out=targ_alt, in0=cstage2, in1=cf_i, op=ALU.mult), ["cst2", "cfi"], ["ta"])
        dve(lambda e: e.tensor_tensor(out=Cre, in0=tq, in1=targ_alt, op=ALU.subtract), ["tq", "ta"], ["Cre"])
        dve(lambda e: e.tensor_tensor(out=tq, in0=cstage, in1=cf_i, op=ALU.mult), ["cst", "cfi", "Cre"], ["tq"])
        dve(lambda e: e.tensor_tensor(out=targ_alt, in0=cstage2, in1=cf_r, op=ALU.mult), ["cst2", "cfr", "Cre"], ["ta"])
        dve(lambda e: e.tensor_tensor(out=tq, in0=tq, in1=targ_alt, op=ALU.add), ["tq", "ta"], ["tq"])
        dve(lambda e: e.tensor_scalar(out=nCim, in0=tq, scalar1=-1.0, scalar2=None, op0=ALU.mult), ["tq"], ["nCim"])
        S.barrier_all()

        wb_state = {"i": 0, "pm": 0}
        s5v = lambda i, shape: A.view(s5w[i][1], shape, F32)
        h0r_v = s5v(4, (32, 16)); h0i_v = s5v(5, (32, 16))
        tA_v = s5v(0, (32, 16)); tB_v = s5v(1, (32, 16)); tC_v = s5v(2, (32, 16)); tD_v = s5v(3, (32, 16))

        def bc3(small_ap, n):
            return small_ap.unsqueeze(2).to_broadcast([128, 32, n])

        def run_pass(kind, x_src, tok0, out_dst):
            ntok = 16 if kind == "samp" else 512
            ntile = 1 if kind == "samp" else 4
            rows = 16 if kind == "samp" else 128
            samp = kind == "samp"
            hTp = hTs if samp else hT
            yTp = yTs if samp else yT
            cur["yT"] = yTp
            ring = wblk_samp if samp else wblk
            depth = len(ring) - 1
            oTsel = (lambda f: oTs[:, f, :]) if samp else (lambda f: oT_lo[:, f, :] if f < 8 else oT_hi[:, f - 8, :])
            xbufs = [(xt, "xt"), (outt, "outt")]
            v4 = lambda t_: t_[:].rearrange("p (a b) -> p a b", a=4)
            tbanks = [(v4(pm[0]), "pm0"), (v4(pm[1]), "pm1"), (v4(px), "px"), (pyg[:, 0:4, :], "pygb0")]

            def load_x(t):
                buf, key = xbufs[t % 2]
                r0 = tok0 + t * 128
                S.dma("sp", lambda e, r0=r0, buf=buf: e.dma_start(out=buf[0:rows], in_=x_src[r0:r0 + rows, :]), [], [key])

            def phaseA(t):
                xb_, xk_ = xbufs[t % 2]
                o = 4 * (t % 2)
                sk = f"st4_{t % 2}"
                dve(lambda e: e.memset(st4[:, o:o + 4], 0.0), [], [sk])
                act(lambda e: e.activation(out=xn[0:rows], in_=xb_[0:rows], func=AF.Square, accum_out=st4[0:rows, o:o + 1]),
                    [xk_, sk], ["xn", sk, "rstd", "zt"])
                act(lambda e: e.activation(out=st4[0:rows, o + 1:o + 2], in_=st4[0:rows, o:o + 1], func=AF.Sqrt, bias=EPS, scale=1.0 / D),
                    [sk], [sk])
                dve(lambda e: e.reciprocal(out=st4[0:rows, o + 2:o + 3], in_=st4[0:rows, o + 1:o + 2]), [sk], [sk])
                act(lambda e: e.activation(out=xb_[0:rows], in_=xb_[0:rows], func=AF.Copy, scale=st4[0:rows, o + 2:o + 3]),
                    [xk_, sk], [xk_])

            def phaseB(t):
                xb_, xk_ = xbufs[t % 2]
                for g4 in range(4):
                    tb, tkey = tbanks[g4]
                    S.pe([lambda e, kc=g4 * 4 + i, i=i, tb=tb, xb_=xb_: e.transpose(tb[:, i, 0:rows], xb_[0:rows, kc * 128:(kc + 1) * 128],
                                                                                   identf[0:rows, 0:rows]) for i in range(4)],
                         [xk_, "identf"], [tkey])
                    for i in range(4):
                        kc = g4 * 4 + i
                        if not samp:
                            if i % 2 == 0:
                                act(lambda e, kc=kc, i=i, t=t, tb=tb: e.activation(out=hTp[:, kc, t * 128:(t + 1) * 128], in_=tb[:, i, :],
                                                                                   func=AF.Identity, scale=Afm[:, kc, 32:33],
                                                                                   bias=Shfm[:, kc, 32:33]), [tkey, "ada"], ["hT"])
                            else:
                                dve(lambda e, kc=kc, i=i, t=t, tb=tb: e.tensor_scalar(out=hTp[:, kc, t * 128:(t + 1) * 128], in0=tb[:, i, :],
                                                                                      scalar1=Afm[:, kc, 32:33], scalar2=Shfm[:, kc, 32:33],
                                                                                      op0=ALU.mult, op1=ALU.add), [tkey, "ada"], ["hT"])
                        else:
                            dve(lambda e, kc=kc, i=i, tb=tb: e.tensor_tensor(out=tmp128[:, 0:16], in0=tb[:, i, 0:16],
                                                                             in1=Afm[:, kc, 0:16], op=ALU.mult), [tkey, "ada"], ["tmp128"])
                            dve(lambda e, kc=kc: e.tensor_tensor(out=hTp[:, kc, 0:16], in0=tmp128[:, 0:16],
                                                                 in1=Shfm[:, kc, 0:16], op=ALU.add), ["tmp128", "ada"], ["hT"])

            load_x(0)
            if ntile > 1:
                load_x(1)
            phaseA(0)
            for t in range(ntile):
                if t + 1 < ntile:
                    phaseA(t + 1)
                phaseB(t)
                if t + 2 < ntile:
                    load_x(t + 2)
            blocks = []
            if kind != "prev":
                for b in range(8):
                    blocks.append(("v", w_in_v, 1024 + b * 128, b))
                for h in range(8):
                    blocks.append(("u", w_in_v, h * 128, h))
                    blocks.append(("za", w_in_v, 2048 + h * 128, h))
            for j in range(8):
                blocks.append(("xb", w_in_v, 3072 + j * 128, j))
                if kind != "prev":
                    blocks.append(("zb", w_in_v, 4096 + j * 128, j))
            if kind != "prev":
                for f in range(16):
                    blocks.append(("o", w_out_v, f * 128, f))
            loaded = {}

            def issue(bi):
                if bi >= len(blocks) or bi in loaded:
                    return
                _, src, col0, _ = blocks[bi]
                i = bi
                buf = ring[i % len(ring)]; key = f"wblk{i % len(ring)}"
                S.dma("pool", lambda e, buf=buf, src=src, col0=col0: e.dma_start(out=buf, in_=src[:, :, col0:col0 + 128]),
                      [], [key])
                loaded[bi] = (buf, key)

            for b0 in range(depth):
                issue(b0)
            for bi, (bk, src, col0, idx) in enumerate(blocks):
                issue(bi + depth)
                buf, wkey = loaded[bi]
                pmi = wb_state["pm"] % 2; wb_state["pm"] += 1
                pmb = pm[pmi]; pkey = f"pm{pmi}"
                if bk == "v":
                    for t in range(ntile):
                        S.pe([lambda e, kc=kc, t=t, buf=buf, pmb=pmb: e.matmul(pmb[0:rows, t * 128:(t + 1) * 128],
                                                                            lhsT=hTp[:, kc, t * 128:t * 128 + rows],
                                                                            rhs=buf[:, kc, :], start=(kc == 0), stop=(kc == 15))
                              for kc in range(16)], [wkey, "hT"], [pkey])
                    act(lambda e, pmb=pmb, idx=idx: e.activation(
                        out=vraw[0:rows, 0:ntile, idx * 128:(idx + 1) * 128],
                        in_=pmb[0:rows, 0:ntile * 128].rearrange("p (a b) -> p a b", a=ntile), func=AF.Copy),
                        [pkey], ["vraw"])
                    if idx == 7:
                        for t in range(ntile):
                            vr = vraw[0:rows, t, :]
                            dve(lambda e: e.memset(st4, 0.0), [], ["st4"])
                            act(lambda e, vr=vr: e.activation(out=outt[0:rows, 0:1024], in_=vr, func=AF.Identity,
                                                              accum_out=st4[0:rows, 0:1]), ["vraw", "st4"], ["outt", "st4"])
                            act(lambda e, vr=vr: e.activation(out=outt[0:rows, 0:1024], in_=vr, func=AF.Square,
                                                              accum_out=st4[0:rows, 1:2]), ["vraw", "st4"], ["outt", "st4"])
                            dve(lambda e: e.tensor_scalar(out=st4[0:rows, 2:3], in0=st4[0:rows, 0:1], scalar1=1.0 / 1024,
                                                          scalar2=None, op0=ALU.mult), ["st4"], ["st4"])
                            dve(lambda e: e.tensor_tensor(out=st4[0:rows, 3:4], in0=st4[0:rows, 2:3], in1=st4[0:rows, 2:3],
                                                          op=ALU.mult), ["st4"], ["st4"])
                            dve(lambda e: e.scalar_tensor_tensor(out=st4[0:rows, 4:5], in0=st4[0:rows, 1:2], scalar=1.0 / 1024,
                                                                 in1=st4[0:rows, 3:4], op0=ALU.mult, op1=ALU.subtract),
                                ["st4"], ["st4"])
                            act(lambda e: e.activation(out=st4[0:rows, 5:6], in_=st4[0:rows, 4:5], func=AF.Sqrt, bias=EPS, scale=1.0),
                                ["st4"], ["st4"])
                            dve(lambda e: e.reciprocal(out=st4[0:rows, 6:7], in_=st4[0:rows, 5:6]), ["st4"], ["st4"])
                            dve(lambda e: e.scalar_tensor_tensor(out=st4[0:rows, 7:8], in0=st4[0:rows, 2:3], scalar=-1.0,
                                                                 in1=st4[0:rows, 6:7], op0=ALU.mult, op1=ALU.mult),
                                ["st4"], ["st4"])
                            act(lambda e, vr=vr: e.activation(out=vr, in_=vr, func=AF.Identity, scale=st4[0:rows, 6:7],
                                                              bias=st4[0:rows, 7:8]), ["vraw", "st4"], ["vraw"])
                            dve(lambda e, vr=vr: e.tensor_tensor(out=vr, in0=vr, in1=lng[0:rows], op=ALU.mult), ["vraw", "lng"], ["vraw"])
                            dve(lambda e, vr=vr: e.tensor_tensor(out=vr, in0=vr, in1=lnb[0:rows], op=ALU.add), ["vraw", "lnb"], ["vraw"])
                        if samp:
                            S.dma("sp", lambda e: e.dma_start(out=vs_o, in_=vraw[0:16, 0, :]), ["vraw"], [], is_out=True)
                    continue
                rhs_src = yTp if bk == "o" else hTp
                rkey = "yT" if bk == "o" else "hT"
                S.pe([lambda e, kc=kc, buf=buf, pmb=pmb, rhs_src=rhs_src: e.matmul(pmb[:, 0:ntok], lhsT=buf[:, kc, :],
                                                                                  rhs=rhs_src[:, kc, 0:ntok],
                                                                                  start=(kc == 0), stop=(kc == 15))
                      for kc in range(16)], [wkey, rkey], [pkey])
                if bk == "u":
                    act(lambda e, pmb=pmb: e.activation(out=u_sb[:, 0:ntok], in_=pmb[:, 0:ntok], func=AF.Copy), [pkey], ["u_sb"])
                elif bk == "za":
                    h = idx
                    act(lambda e, pmb=pmb: e.activation(out=sza[:, 0:ntok], in_=pmb[:, 0:ntok], func=AF.Silu), [pkey], ["sza"])
                    if not samp:
                        S.pe([lambda e, t=t, h=h: e.matmul(px[:, t * 128:(t + 1) * 128], lhsT=vraw[:, t, h * 128:(h + 1) * 128],
                                                           rhs=Wt[:, h, :], start=True, stop=True) for t in range(4)],
                             ["vraw", "Wt"], ["px"])
                        dve(lambda e, h=h: e.tensor_tensor(out=zt.rearrange("p (a b) -> p a b", a=4),
                                                           in0=px[:].rearrange("p (a b) -> p a b", a=4),
                                                           in1=Bsbc[:, h, :].unsqueeze(1).to_broadcast([128, 4, 128]), op=ALU.add),
                            ["px", "Bsbc"], ["zt"])
                    else:
                        S.pe([lambda e, h=h: e.matmul(px[:, 0:16], lhsT=vraw[0:16, 0, h * 128:(h + 1) * 128],
                                                      rhs=Wsamp[0:16, h, :], start=True, stop=True)],
                             ["vraw", "Wsamp"], ["px"])
                        dve(lambda e, h=h: e.tensor_scalar(out=zt[:, 0:16], in0=px[:, 0:16], scalar1=bs0bc[:, h:h + 1],
                                                           scalar2=None, op0=ALU.add), ["px", "bs0bc"], ["zt"])
                    dve(lambda e: e.tensor_tensor(out=zt[:, 0:ntok], in0=zt[:, 0:ntok], in1=u_sb[:, 0:ntok], op=ALU.mult),
                        ["zt", "u_sb"], ["zt"])
                    dve(lambda e, h=h: e.tensor_tensor(out=yTp[:, h, 0:ntok], in0=zt[:, 0:ntok], in1=sza[:, 0:ntok], op=ALU.mult),
                        ["zt", "sza"], ["yT"])
                elif bk == "xb":
                    act(lambda e, pmb=pmb, idx=idx: e.activation(out=xbT2[idx % 2][:, 0:ntok], in_=pmb[:, 0:ntok], func=AF.Copy),
                        [pkey], [f"xbT{idx % 2}", "u_sb", "sza"])
                    if idx == 0 and kind == "own":
                        build_rhoZ()
                    if kind == "prev":
                        if idx == 0:
                            dve(lambda e: e.memset(accs, 0.0), [], ["accs"])
                        if idx >= 1:
                            prev_unit(idx - 1)
                        if idx == 7:
                            prev_unit(7)
                            prev_combine()
                elif bk == "zb":
                    act(lambda e, pmb=pmb, idx=idx: e.activation(out=szb2[idx % 2][:, 0:ntok], in_=pmb[:, 0:ntok], func=AF.Silu),
                        [pkey], [f"szb{idx % 2}", "u_sb", "sza"])
                    if samp:
                        if idx == 0:
                            samp_prep()
                        s5_unit_samp(idx)
                        if idx == 7:
                            samp_finish()
                    else:
                        if idx >= 1:
                            s5_unit(idx - 1, True, ntok)
                            flush_pend()
                        if idx == 7:
                            s5_unit(7, True, ntok)
                            flush_pend()
                elif bk == "o":
                    f = idx
                    oTf = oTsel(f)
                    act(lambda e, pmb=pmb, oTf=oTf: e.activation(out=oTf[:, 0:ntok], in_=pmb[:, 0:ntok], func=AF.Copy), [pkey], ["oT", "vraw", "u_sb"])
                    act(lambda e, pmb=pmb: e.activation(out=sqt[:, 0:ntok], in_=pmb[:, 0:ntok], func=AF.Square), [pkey], ["sqt"])
                    S.pe([lambda e, f=f: e.matmul(pssq[:, 0:ntok], lhsT=ones_f, rhs=sqt[:, 0:ntok], start=(f == 0), stop=(f == 15))],
                         ["sqt", "ones_f"], ["pb"])
            if kind == "prev":
                S.barrier_all()
                return
            act(lambda e: e.activation(out=rstd_bc[:, 0:ntok], in_=pssq[:, 0:ntok], func=AF.Sqrt, bias=EPS, scale=1.0 / D),
                ["pb"], ["rstd"])
            dve(lambda e: e.reciprocal(out=rstd_bc[:, 0:ntok], in_=rstd_bc[:, 0:ntok]), ["rstd"], ["rstd"])
            ebufs = [(xt, "xt"), (outt, "outt")]
            zbufs = [(zt, "zt"), (sza, "sza")]
            ebanks = [(px, "px"), (pm[0], "pm0"), (pm[1], "pm1"), (pyg[:, 0:4, :].rearrange("p a b -> p (a b)"), "pygb0")]

            def load_xe(t):
                buf, key = ebufs[t % 2]
                r0 = tok0 + t * 128
                S.dma("sp", lambda e, r0=r0, buf=buf: e.dma_start(out=buf[0:rows], in_=x_src[r0:r0 + rows, :]), [], [key])

            load_xe(0)
            for t in range(ntile):
                if t + 1 < ntile:
                    load_xe(t + 1)
                r0 = tok0 + t * 128
                ebuf, ekey = ebufs[t % 2]
                for g4 in range(4):
                    zb_, zk_ = zbufs[g4 % 2]
                    pb_, pk_ = ebanks[g4]
                    for i in range(4):
                        f = g4 * 4 + i
                        oTf = oTsel(f)
                        c0 = t * 128
                        if not samp:
                            dve(lambda e, oTf=oTf, f=f, i=i, c0=c0, zb_=zb_: e.scalar_tensor_tensor(
                                out=zb_[:, i * 128:(i + 1) * 128], in0=oTf[:, c0:c0 + 128], scalar=GGfm[:, f, 32:33],
                                in1=rstd_bc[:, c0:c0 + 128], op0=ALU.mult, op1=ALU.mult), ["oT", "rstd", "ada"], [zk_])
                        else:
                            dve(lambda e, oTf=oTf, i=i, zb_=zb_: e.tensor_tensor(out=zb_[:, i * 128:i * 128 + 16], in0=oTf[:, 0:16],
                                                                                 in1=rstd_bc[:, 0:16], op=ALU.mult), ["oT", "rstd"], [zk_])
                            dve(lambda e, f=f, i=i, zb_=zb_: e.tensor_tensor(out=zb_[:, i * 128:i * 128 + 16], in0=zb_[:, i * 128:i * 128 + 16],
                                                                             in1=GGfm[:, f, 0:16], op=ALU.mult), [zk_, "ada"], [zk_])
                    S.pe([lambda e, i=i, zb_=zb_, pb_=pb_: e.transpose(pb_[0:rows, i * 128:(i + 1) * 128], zb_[:, i * 128:i * 128 + rows], identf)
                          for i in range(4)], [zk_, "identf"], [pk_])
                    dve(lambda e, g4=g4, pb_=pb_, ebuf=ebuf: e.tensor_tensor(out=ebuf[0:rows, g4 * 512:(g4 + 1) * 512], in0=pb_[0:rows, :],
                                                                            in1=ebuf[0:rows, g4 * 512:(g4 + 1) * 512], op=ALU.add),
                        [pk_, ekey], [ekey])
                S.dma("sp", lambda e, r0=r0, ebuf=ebuf: e.dma_start(out=out_dst[r0:r0 + rows, :], in_=ebuf[0:rows]), [ekey], [], is_out=True)
            S.barrier_all()

        def build_rhoZ():
            dve(lambda e: e.tensor_copy(out=rhoZ, in_=bc3(sm["rho"], 128)), ["rho"], ["vraw"])
            dve(lambda e: e.memset(rhoZ[:, :, 0:1], 0.0), ["vraw"], ["vraw"])

        def s5_stage1(j, c0, n, pp):
            xb = xbT2[j % 2]
            dve(lambda e: e.scalar_tensor_tensor(out=ybf2[pp][:, 0:n], in0=xb[:, c0:c0 + n], scalar=dfm[:, j:j + 1],
                                                 in1=pyg[:, 4 * pp, 0:n], op0=ALU.mult, op1=ALU.add),
                [f"xbT{j % 2}", "dfm", f"pygb{pp}"], [f"ybf{pp}"])
            S.pe([lambda e: e.matmul(pyg[:, 4 * pp + 1, 0:n], lhsT=Wgv[:, j, :], rhs=ybf2[pp][:, 0:n], start=True, stop=True),
                  lambda e: e.matmul(pyg[:, 4 * pp + 2, 0:n], lhsT=Wgg[:, j, :], rhs=ybf2[pp][:, 0:n], start=True, stop=True)],
                 [f"ybf{pp}", "Wgv", "Wgg"], [f"pygb{pp}"])
            act(lambda e: e.activation(out=sig2[pp][:, 0:n], in_=pyg[:, 4 * pp + 2, 0:n], func=AF.Sigmoid, bias=bgg[:, j:j + 1], scale=1.0),
                [f"pygb{pp}", "bgg"], [f"sig{pp}"])

        def s5_stage2(j, c0, n, pp):
            sz = szb2[j % 2]
            dve(lambda e: e.scalar_tensor_tensor(out=tmp128[:, 0:n], in0=pyg[:, 4 * pp + 1, 0:n], scalar=bgv[:, j:j + 1],
                                                 in1=sig2[pp][:, 0:n], op0=ALU.add, op1=ALU.mult),
                [f"pygb{pp}", "bgv", f"sig{pp}"], ["tmp128"])
            dve(lambda e, ydst=cur["yT"]: e.tensor_tensor(out=ydst[:, 8 + j, c0:c0 + n], in0=tmp128[:, 0:n], in1=sz[:, c0:c0 + n], op=ALU.mult),
                ["tmp128", f"szb{j % 2}"], ["yT"])

        P1, P2, P3, P4 = hrb2[0], hrb2[1], hib2[0], hib2[1]

        def s5_cproj(j, n, pp):
            fns = []
            for kk in range(4):
                k = 4 * j + kk
                for wi_, (wgt, src) in enumerate(((Cre, P1), (Cre, P2), (nCim, P3), (nCim, P4))):
                    fns.append(lambda e, k=k, kk=kk, wgt=wgt, src=src, first=(kk == 0 and wi_ == 0), last=(kk == 3 and wi_ == 3):
                               e.matmul(pyg[:, 4 * pp, 0:n], lhsT=wgt[:, k, :], rhs=src[:, kk, 0:n], start=first, stop=last))
            S.pe(fns, ["P", "Cre", "nCim"], [f"pygb{pp}"])

        def s5_cproj_samp(j, n):
            fns = []
            for kk in range(4):
                k = 4 * j + kk
                fns.append(lambda e, k=k, kk=kk: e.matmul(pyg[:, 0, 0:n], lhsT=Cre[:, k, :], rhs=hrb[:, kk, 0:n], start=(kk == 0), stop=False))
                fns.append(lambda e, k=k, kk=kk: e.matmul(pyg[:, 0, 0:n], lhsT=nCim[:, k, :], rhs=hib[:, kk, 0:n], start=False, stop=(kk == 3)))
            S.pe(fns, ["P", "Cre", "nCim"], ["pygb0"])

        def s5_out(j, c0, n, hr_src, hi_src):
            s5_cproj_samp(j, n)
            s5_stage1(j, c0, n, 0)
            s5_stage2(j, c0, n, 0)

        pend = []

        def advance_pend(new_item=None):
            nxt = []
            for (stage, pj, pc0, ppp) in pend:
                if stage == 1:
                    s5_stage1(pj, pc0, 128, ppp)
                    nxt.append((2, pj, pc0, ppp))
                else:
                    s5_stage2(pj, pc0, 128, ppp)
            pend.clear()
            pend.extend(nxt)
            if new_item is not None:
                pend.append(new_item)

        def flush_pend():
            while pend:
                advance_pend()

        def s5_unit(j, full, ntok):
            Ct_j = Ct[:, 4 * j:4 * j + 4, :]
            St_j = St[:, 4 * j:4 * j + 4, :]
            rz = rhoZ[:, 4 * j:4 * j + 4, :].rearrange("p a b -> p (a b)")
            js = slice(4 * j, 4 * j + 4)
            flat = lambda t: t.rearrange("p a b -> p (a b)")
            xb = xbT2[j % 2]
            xkey = f"xbT{j % 2}"
            def emit_B(c):
                c0 = c * 128
                fns = []
                for kk in range(4):
                    k = 4 * j + kk
                    fns.append(lambda e, k=k, kk=kk, c0=c0: e.matmul(pbr[:, kk, :], lhsT=Bre[:, k, :], rhs=xb[:, c0:c0 + 128], start=True, stop=True))
                    fns.append(lambda e, k=k, kk=kk, c0=c0: e.matmul(pbi[:, kk, :], lhsT=Bim[:, k, :], rhs=xb[:, c0:c0 + 128], start=True, stop=True))
                S.pe(fns, [xkey, "Bre", "Bim"], ["pb"])

            emit_B(0)
            for c in range(4):
                c0 = c * 128
                pp = c % 2
                dve(lambda e: e.tensor_tensor(out=tA, in0=Ct_j, in1=pbr[:], op=ALU.mult), ["pb", "Ct"], ["tA"])
                dve(lambda e: e.tensor_tensor(out=tB, in0=St_j, in1=pbi[:], op=ALU.mult), ["pb", "St"], ["tB"])
                dve(lambda e: e.tensor_tensor(out=tC, in0=Ct_j, in1=pbi[:], op=ALU.mult), ["pb", "Ct"], ["tC"])
                dve(lambda e: e.tensor_tensor(out=tD, in0=St_j, in1=pbr[:], op=ALU.mult), ["pb", "St"], ["tD"])
                if c < 3:
                    emit_B(c + 1)
                dve(lambda e: e.tensor_tensor(out=sm["p5"][:, 0:4], in0=sm["rho"][:, js], in1=gin2r[:, j, 0:4], op=ALU.mult), ["rho", "gin"], ["p5"])
                dve(lambda e: e.tensor_tensor(out=sm["p6"][:, 0:4], in0=sm["rho"][:, js], in1=gin2i[:, j, 0:4], op=ALU.mult), ["rho", "gin"], ["p6"])
                dve(lambda e: e.tensor_tensor(out=tA, in0=tA, in1=tB, op=ALU.add), ["tA", "tB"], ["tA"])
                dve(lambda e: e.tensor_tensor(out=tC, in0=tC, in1=tD, op=ALU.subtract), ["tC", "tD"], ["tC"])
                dve(lambda e: e.tensor_tensor(out=tA[:, :, 0], in0=tA[:, :, 0], in1=sm["p5"][:, 0:4], op=ALU.add), ["tA", "p5"], ["tA"])
                dve(lambda e: e.tensor_tensor(out=tC[:, :, 0], in0=tC[:, :, 0], in1=sm["p6"][:, 0:4], op=ALU.add), ["tC", "p6"], ["tC"])
                dve(lambda e: e.tensor_tensor_scan(out=flat(gr), data0=rz, data1=flat(tA), initial=0.0, op0=ALU.mult, op1=ALU.add),
                    ["tA", "vraw"], ["gr"])
                dve(lambda e: e.tensor_tensor_scan(out=flat(gi), data0=rz, data1=flat(tC), initial=0.0, op0=ALU.mult, op1=ALU.add),
                    ["tC", "vraw"], ["gi"])
                grl = gr[:, :, 127]; gil = gi[:, :, 127]
                if full:
                    dve(lambda e: e.tensor_tensor(out=P1, in0=Ct_j, in1=gr, op=ALU.mult), ["gr", "Ct"], ["P"])
                    dve(lambda e: e.scalar_tensor_tensor(out=P2, in0=gi, scalar=-1.0, in1=St_j, op0=ALU.mult, op1=ALU.mult), ["gi", "St"], ["P"])
                    dve(lambda e: e.tensor_tensor(out=P3, in0=St_j, in1=gr, op=ALU.mult), ["gr", "St"], ["P"])
                    dve(lambda e: e.tensor_tensor(out=P4, in0=Ct_j, in1=gi, op=ALU.mult), ["gi", "Ct"], ["P"])
                dve(lambda e: e.tensor_tensor(out=sm["p1"][:, 0:4], in0=sm["eTr"][:, js], in1=grl, op=ALU.mult), ["gr", "eTc"], ["p1"])
                dve(lambda e: e.tensor_tensor(out=sm["p2"][:, 0:4], in0=sm["eTi"][:, js], in1=gil, op=ALU.mult), ["gi", "eTs"], ["p2"])
                dve(lambda e: e.tensor_tensor(out=sm["p3"][:, 0:4], in0=sm["eTi"][:, js], in1=grl, op=ALU.mult), ["gr", "eTs"], ["p3"])
                dve(lambda e: e.tensor_tensor(out=sm["p4"][:, 0:4], in0=sm["eTr"][:, js], in1=gil, op=ALU.mult), ["gi", "eTc"], ["p4"])
                dve(lambda e: e.tensor_tensor(out=gin2r[:, j, 0:4], in0=sm["p1"][:, 0:4], in1=sm["p2"][:, 0:4], op=ALU.subtract),
                    ["p1", "p2", "gin"], ["gin"])
                dve(lambda e: e.tensor_tensor(out=gin2i[:, j, 0:4], in0=sm["p3"][:, 0:4], in1=sm["p4"][:, 0:4], op=ALU.add),
                    ["p3", "p4", "gin"], ["gin"])
                if not full:
                    continue
                s5_cproj(j, 128, pp)
                advance_pend((1, j, c0, pp))

        def cmul(out_r, out_i, ar, ai, br, bi, t1, t2, keys_r, key_w):
            dve(lambda e: e.tensor_tensor(out=t1, in0=ar, in1=br, op=ALU.mult), keys_r, [key_w + "t1"])
            dve(lambda e: e.tensor_tensor(out=t2, in0=ai, in1=bi, op=ALU.mult), keys_r, [key_w + "t2"])
            dve(lambda e: e.tensor_tensor(out=out_r, in0=t1, in1=t2, op=ALU.subtract), [key_w + "t1", key_w + "t2"], [key_w + "r"])
            dve(lambda e: e.tensor_tensor(out=t1, in0=ar, in1=bi, op=ALU.mult), keys_r + [key_w + "r"], [key_w + "t1"])
            dve(lambda e: e.tensor_tensor(out=t2, in0=ai, in1=br, op=ALU.mult), keys_r + [key_w + "r"], [key_w + "t2"])
            dve(lambda e: e.tensor_tensor(out=out_i, in0=t1, in1=t2, op=ALU.add), [key_w + "t1", key_w + "t2"], [key_w + "i"])

        def samp_prep():
            S.dma("sp", lambda e: e.dma_start(out=sre_t, in_=sre_d), [], ["sre_t", "vraw"])
            S.dma("sp", lambda e: e.dma_start(out=sim_t, in_=sim_d), [], ["sim_t", "xt", "outt"])
            for src_t, dst, skey in ((sre_t, h0r_v, "sre_t"), (sim_t, h0i_v, "sim_t")):
                for k8 in range(4):
                    S.pe([lambda e, k=k8 * 8 + i, i=i, src_t=src_t: e.transpose(px[:, i * 16:(i + 1) * 16], src_t[0:16, k * 128:(k + 1) * 128],
                                                                                identf[0:16, 0:16]) for i in range(8)],
                         [skey, "identf"], ["px"])
                    dve(lambda e, k8=k8, dst=dst: e.tensor_copy(out=dst[:, k8 * 8:(k8 + 1) * 8, :],
                                                                in_=px[:, 0:128].rearrange("p (a b) -> p a b", a=8)), ["px"], ["h0"])
            cmul(tA_v, tB_v, bc3(sm["icr"], 16), bc3(sm["ici"], 16), h0r_v, h0i_v, tC_v, tD_v, ["h0", "icr", "ici"], "h0p")
            cmul(tC_v, tD_v, bc3(sm["abr"], 16), bc3(sm["abi"], 16), tA_v, tB_v, h0r_v, h0i_v, ["h0pr", "h0pi", "abr", "abi"], "ah")

        def s5_unit_samp(j):
            fns = []
            for kk in range(4):
                k = 4 * j + kk
                fns.append(lambda e, k=k, kk=kk: e.matmul(pbr[:, kk, 0:16], lhsT=Bre[:, k, :], rhs=xbT2[j % 2][:, 0:16], start=True, stop=True))
                fns.append(lambda e, k=k, kk=kk: e.matmul(pbi[:, kk, 0:16], lhsT=Bim[:, k, :], rhs=xbT2[j % 2][:, 0:16], start=True, stop=True))
            S.pe(fns, [f"xbT{j % 2}", "Bre", "Bim"], ["pb"])
            js = slice(4 * j, 4 * j + 4)
            dve(lambda e: e.tensor_tensor(out=tC_v[:, js, :], in0=tC_v[:, js, :], in1=pbr[:, :, 0:16], op=ALU.add), ["pb", "ahr"], ["hp"])
            dve(lambda e: e.tensor_tensor(out=tD_v[:, js, :], in0=tD_v[:, js, :], in1=pbi[:, :, 0:16], op=ALU.add), ["pb", "ahi"], ["hp"])
            dve(lambda e: e.tensor_copy(out=hrb[:, :, 0:16], in_=tC_v[:, js, :]), ["hp"], ["P"])
            dve(lambda e: e.tensor_copy(out=hib[:, :, 0:16], in_=tD_v[:, js, :]), ["hp"], ["P"])
            s5_out(j, 0, 16, hrb, hib)

        def samp_finish():
            cmul(tA_v, tB_v, bc3(sm["cfr"], 16), bc3(sm["cfi"], 16), tC_v, tD_v, h0r_v, h0i_v, ["hp", "cfr", "cfi"], "hs")
            for src_v, dst_t, dram, skey in ((tA_v, sre_t, hsre_o, "hsr"), (tB_v, sim_t, hsim_o, "hsi")):
                for k4 in range(8):
                    S.pe([lambda e, k=k4 * 4 + i, i=i, src_v=src_v: e.transpose(px[0:16, i * 128:(i + 1) * 128], src_v[:, k, :], identf)
                          for i in range(4)], [skey, "identf"], ["px"])
                    dve(lambda e, k4=k4, dst_t=dst_t: e.tensor_copy(out=dst_t[0:16, k4 * 512:(k4 + 1) * 512], in_=px[0:16, :]),
                        ["px"], ["hstok" + skey, "vraw", "xt", "outt"])
                S.dma("sp", lambda e, dst_t=dst_t, dram=dram: e.dma_start(out=dram, in_=dst_t), ["hstok" + skey, "vraw", "xt", "outt"], [], is_out=True)

        dve(lambda e: e.tensor_tensor(out=sm["p1"], in0=sm["dt"], in1=sm["are"], op=ALU.mult), [], ["p1"])
        dve(lambda e: e.tensor_scalar(out=tauR, in0=tauf, scalar1=-1.0, scalar2=127.0, op0=ALU.mult, op1=ALU.add), [], ["tauR"])
        dve(lambda e: e.tensor_tensor(out=targ, in0=bc3(sm["p1"], 128), in1=tauR.unsqueeze(1).to_broadcast([128, 32, 128]), op=ALU.mult),
            ["p1", "tauR"], ["targ"])
        act(lambda e: e.activation(out=targ, in_=targ, func=AF.Exp), ["targ"], ["targ"])
        dve(lambda e: e.tensor_tensor(out=RCt, in0=targ, in1=Ct, op=ALU.mult), ["targ"], ["RCt"])
        dve(lambda e: e.tensor_tensor(out=RSt, in0=targ, in1=St, op=ALU.mult), ["targ"], ["RSt"])
        act(lambda e: e.activation(out=sm["p2"], in_=sm["p1"], func=AF.Exp, scale=128.0), ["p1"], ["p2"])
        dve(lambda e: e.tensor_tensor(out=A128r, in0=sm["p2"], in1=sm["eTr"], op=ALU.mult), ["p2"], ["A128r"])
        dve(lambda e: e.tensor_tensor(out=A128i, in0=sm["p2"], in1=sm["eTi"], op=ALU.mult), ["p2"], ["A128i"])
        dve(lambda e: e.memset(sm["ginr"], 0.0), [], ["H"])
        dve(lambda e: e.memset(sm["gini"], 0.0), ["H"], ["H"])
        S.barrier_all()

        def prev_unit(j):
            xb = xbT2[j % 2]
            xkey = f"xbT{j % 2}"

            def emit_B(c):
                c0 = c * 128
                fns = []
                for kk in range(4):
                    k = 4 * j + kk
                    fns.append(lambda e, k=k, kk=kk, c0=c0: e.matmul(pbr[:, kk, :], lhsT=Bre[:, k, :], rhs=xb[:, c0:c0 + 128], start=True, stop=True))
                    fns.append(lambda e, k=k, kk=kk, c0=c0: e.matmul(pbi[:, kk, :], lhsT=Bim[:, k, :], rhs=xb[:, c0:c0 + 128], start=True, stop=True))
                S.pe(fns, [xkey, "Bre", "Bim"], ["pb"])

            emit_B(0)
            for c in range(4):
                for kk in range(4):
                    k = 4 * j + kk
                    for comp, (tbl, src, sgn) in enumerate(((RCt, pbr, 1.0), (RSt, pbi, 1.0), (RCt, pbi, 1.0), (RSt, pbr, 1.0))):
                        dve(lambda e, k=k, kk=kk, c=c, comp=comp, tbl=tbl, src=src, sgn=sgn: e.scalar_tensor_tensor(
                            out=junkR, in0=tbl[:, k, :], scalar=sgn, in1=src[:, kk, :], op0=ALU.mult, op1=ALU.mult,
                            accum_out=accs[:, c, k, comp:comp + 1]),
                            ["pb", "RCt", "RSt"] + (["accs"] if (j == 0 and c == 0 and kk == 0 and comp == 0) else []),
                            (["accs"] if (j == 7 and c == 3 and kk == 3 and comp == 3) else []))
                if c < 3:
                    emit_B(c + 1)

        def prev_combine():
            H_r, H_i = sm["ginr"], sm["gini"]
            rot_r, rot_i = Ct[:, :, 127], St[:, :, 127]
            for c in range(4):
                dve(lambda e, c=c: e.tensor_tensor(out=sm["p1"], in0=accs[:, c, :, 0], in1=accs[:, c, :, 1], op=ALU.add), ["accs"], ["p1"])
                dve(lambda e, c=c: e.tensor_tensor(out=sm["p2"], in0=accs[:, c, :, 2], in1=accs[:, c, :, 3], op=ALU.subtract), ["accs"], ["p2"])
                cmul(sm["p3"], sm["p4"], rot_r, rot_i, sm["p1"], sm["p2"], sm["p5"], sm["p6"], ["p1", "p2"], "rE")
                cmul(sm["p1"], sm["p2"], A128r, A128i, H_r, H_i, sm["p5"], sm["p6"], ["H", "rEr", "rEi"], "aH")
                dve(lambda e: e.tensor_tensor(out=H_r, in0=sm["p1"], in1=sm["p3"], op=ALU.add), ["aHr", "rEr", "H"], ["H"])
                dve(lambda e: e.tensor_tensor(out=H_i, in0=sm["p2"], in1=sm["p4"], op=ALU.add), ["aHi", "rEi", "H"], ["H"])

        run_pass("prev", xp, 0, None)
        run_pass("prev", xp, 512, None)
        cmul(sm["p1"], sm["p2"], sm["e1r"], sm["e1i"], sm["ginr"], sm["gini"], sm["p3"], sm["p4"], ["H"], "g0")
        dve(lambda e: e.tensor_copy(out=gin2r[:, :, 0:4], in_=sm["p1"].rearrange("p (a b) -> p a b", a=8)), ["g0r"], ["gin"])
        dve(lambda e: e.tensor_copy(out=gin2i[:, :, 0:4], in_=sm["p2"].rearrange("p (a b) -> p a b", a=8)), ["g0i"], ["gin"])
        dve(lambda e: e.tensor_scalar(out=gin2r, in0=gin2r, scalar1=maskv[:, 0:1], scalar2=None, op0=ALU.mult),
            ["gin", "maskv"], ["gin"])
        dve(lambda e: e.tensor_scalar(out=gin2i, in0=gin2i, scalar1=maskv[:, 0:1], scalar2=None, op0=ALU.mult),
            ["gin", "maskv"], ["gin"])
        run_pass("own", xo, 0, yo)
        run_pass("own", xo, 512, yo)
        dve(lambda e: e.tensor_scalar(out=sm["p5"], in0=sm["e1i"], scalar1=-1.0, scalar2=None, op0=ALU.mult), ["e1s"], ["p5"])
        dve(lambda e: e.tensor_copy(out=sm["ginr"].rearrange("p (a b) -> p a b", a=8), in_=gin2r[:, :, 0:4]), ["gin"], ["ginc"])
        dve(lambda e: e.tensor_copy(out=sm["gini"].rearrange("p (a b) -> p a b", a=8), in_=gin2i[:, :, 0:4]), ["gin"], ["ginc"])
        cmul(sm["p1"], sm["p2"], sm["e1r"], sm["p5"], sm["ginr"], sm["gini"], sm["p3"], sm["p4"], ["ginc", "e1c", "p5"], "hf")
        cmul(sm["p5"], sm["p6"], sm["cfr"], sm["cfi"], sm["p1"], sm["p2"], sm["p3"], sm["p4"], ["hfr", "hfi", "cfr", "cfi"], "hF")
        S.pe([lambda e: e.transpose(px[0:32, 0:128], sm["p5"], identf),
              lambda e: e.transpose(px[0:32, 128:256], sm["p6"], identf)], ["hFr", "hFi", "identf"], ["px"])
        dve(lambda e: e.tensor_copy(out=zt[0:32, 0:256], in_=px[0:32, 0:256]), ["px"], ["zt"])
        S.dma("sp", lambda e: e.dma_start(out=hpre_o, in_=zt[0:32, 0:128]), ["zt"], [], is_out=True)
        S.dma("sp", lambda e: e.dma_start(out=hpim_o, in_=zt[0:32, 128:256]), ["zt"], [], is_out=True)
        S.barrier_all()
        run_pass("samp", xs, 0, ys)
        S.finish()
        S.emit(block)
    return nc


_CACHE = {}


def _prep_shared(w_c, b_c, g_pre, w_in, ln_v_g, ln_v_b, w_s, b_s, a_re, a_im, log_dt,
                 b_re, b_im, c_re, c_im, d_skip, w_glu, b_glu, w_out, g_post):
    f = lambda a: np.ascontiguousarray(np.asarray(a, dtype=np.float32))
    sh = {}
    sh["w_c"] = f(w_c); sh["b_c"] = f(b_c).reshape(1, -1)
    sh["gpre_fm"] = f(np.asarray(g_pre).reshape(16, 128).T)
    sh["gpost_fm"] = f(np.asarray(g_post).reshape(16, 128).T)
    sh["w_in"] = f(w_in); sh["w_out"] = f(w_out)
    sh["lng"] = f(ln_v_g).reshape(1, -1); sh["lnb"] = f(ln_v_b).reshape(1, -1)
    ws = np.asarray(w_s, dtype=np.float32)
    sh["ws_t"] = f(ws.transpose(2, 0, 1).reshape(128, 1024))
    sh["bsrow"] = f(b_s).reshape(1, 1024)
    sh["w00"] = f(ws[:, 0, 0]).reshape(1, 8)
    sh["bs0"] = f(np.asarray(b_s)[:, 0]).reshape(1, 8)
    r2 = lambda a: f(np.asarray(a, dtype=np.float32).reshape(32, 2, 64).transpose(1, 2, 0).reshape(128, 32))
    sh["a_re2"] = r2(a_re); sh["a_im2"] = r2(a_im)
    sh["ldt2"] = r2(np.repeat(np.asarray(log_dt, dtype=np.float32)[:, None], 64, axis=1))
    bre = np.asarray(b_re, dtype=np.float32); bim = np.asarray(b_im, dtype=np.float32)
    cre = np.asarray(c_re, dtype=np.float32); cim = np.asarray(c_im, dtype=np.float32)
    Bre_pad = np.zeros((128, 32, 128), np.float32); Bim_pad = np.zeros((128, 32, 128), np.float32)
    Cre_pad = np.zeros((128, 32, 128), np.float32); Cim_pad = np.zeros((128, 32, 128), np.float32)
    for k in range(32):
        for q in range(2):
            g = 2 * k + q
            gl = 2 * (k % 4) + q
            Bre_pad[gl * 16:(gl + 1) * 16, k, q * 64:(q + 1) * 64] = bre[g].T
            Bim_pad[gl * 16:(gl + 1) * 16, k, q * 64:(q + 1) * 64] = bim[g].T
            Cre_pad[q * 64:(q + 1) * 64, k, gl * 16:(gl + 1) * 16] = cre[g].T
            Cim_pad[q * 64:(q + 1) * 64, k, gl * 16:(gl + 1) * 16] = cim[g].T
    sh["Bre_pad"] = Bre_pad.reshape(128, 4096); sh["Bim_pad"] = Bim_pad.reshape(128, 4096)
    sh["Cre_pad"] = Cre_pad.reshape(128, 4096); sh["Cim_pad"] = Cim_pad.reshape(128, 4096)
    sh["dfm"] = f(np.asarray(d_skip).reshape(8, 128).T)
    wg = np.asarray(w_glu, dtype=np.float32); bg = np.asarray(b_glu, dtype=np.float32)
    wgv = np.zeros((128, 8, 128), np.float32); wgg = np.zeros((128, 8, 128), np.float32)
    bgv = np.zeros((128, 8), np.float32); bgg = np.zeros((128, 8), np.float32)
    for j in range(8):
        for gl in range(8):
            g = 8 * j + gl
            wgv[gl * 16:(gl + 1) * 16, j, gl * 16:(gl + 1) * 16] = wg[g, :, 0:16]
            wgg[gl * 16:(gl + 1) * 16, j, gl * 16:(gl + 1) * 16] = wg[g, :, 16:32]
            bgv[gl * 16:(gl + 1) * 16, j] = bg[g, 0:16]
            bgg[gl * 16:(gl + 1) * 16, j] = bg[g, 16:32]
    sh["wgv_pad"] = wgv.reshape(128, 1024); sh["wgg_pad"] = wgg.reshape(128, 1024)
    sh["bgv"] = bgv; sh["bgg"] = bgg
    return sh


def kernel(x_prompt, x_sample, c_prompt, c_sample, state_b_re, state_b_im,
           w_c, b_c, g_pre, w_in, ln_v_g, ln_v_b, w_s, b_s, a_re, a_im, log_dt,
           b_re, b_im, c_re, c_im, d_skip, w_glu, b_glu, w_out, g_post):
    f = lambda a: np.ascontiguousarray(np.asarray(a, dtype=np.float32))
    x_prompt = f(x_prompt); x_sample = f(x_sample); c_prompt = f(c_prompt); c_sample = f(c_sample)
    state_b_re = f(state_b_re); state_b_im = f(state_b_im)
    sh = _prep_shared(w_c, b_c, g_pre, w_in, ln_v_g, ln_v_b, w_s, b_s, a_re, a_im, log_dt,
                      b_re, b_im, c_re, c_im, d_skip, w_glu, b_glu, w_out, g_post)
    in_maps = []
    for i in range(NCORES):
        b, hf = i // 2, i % 2
        m = dict(sh)
        m["xo"] = f(x_prompt[b, hf * 1024:(hf + 1) * 1024])
        m["xp"] = f(x_prompt[b, 0:1024])
        m["xs"] = f(x_sample[i * 16:(i + 1) * 16, 0])
        cs = np.zeros((33, D), np.float32)
        cs[0:16] = c_sample[i * 16:(i + 1) * 16]
        cs[32] = c_prompt[b]
        m["cs"] = cs
        m["sre"] = f(state_b_re[i * 16:(i + 1) * 16].reshape(16, 4096))
        m["sim"] = f(state_b_im[i * 16:(i + 1) * 16].reshape(16, 4096))
        m["maskv"] = np.full((128, 1), float(hf), np.float32)
        in_maps.append(m)
    if "nc" not in _CACHE:
        _CACHE["nc"] = build_nc()
    res = run_bass_kernel_spmd(_CACHE["nc"], in_maps, core_ids=list(range(NCORES)))
    R = res.results
    y_prompt = np.zeros((4, 2048, D), np.float32)
    y_sample = np.zeros((128, 1, D), np.float32)
    v_s = np.zeros((128, 1, 1024), np.float32)
    hp_re = np.zeros((4, 64, 64), np.float32); hp_im = np.zeros((4, 64, 64), np.float32)
    hs_re = np.zeros((128, 64, 64), np.float32); hs_im = np.zeros((128, 64, 64), np.float32)
    unr2 = lambda a: np.asarray(a).reshape(64, 64)
    for i in range(NCORES):
        b, hf = i // 2, i % 2
        y_prompt[b, hf * 1024:(hf + 1) * 1024] = R[i]["yo"]
        y_sample[i * 16:(i + 1) * 16, 0] = R[i]["ys"]
        v_s[i * 16:(i + 1) * 16, 0] = R[i]["vs"]
        hs_re[i * 16:(i + 1) * 16] = np.asarray(R[i]["hs_re"]).reshape(16, 64, 64)
        hs_im[i * 16:(i + 1) * 16] = np.asarray(R[i]["hs_im"]).reshape(16, 64, 64)
        if hf == 1:
            hp_re[b] = unr2(R[i]["hp_re"]); hp_im[b] = unr2(R[i]["hp_im"])
    return (y_prompt, y_sample, v_s, hp_re, hp_im, hs_re, hs_im)
```

```python
import math
from contextlib import ExitStack

import numpy as np
import concourse.bass as bass
import concourse.mybir as mybir
from concourse.bass_utils import run_bass_kernel_spmd

F32 = mybir.dt.float32
BF16 = mybir.dt.bfloat16
I32 = mybir.dt.int32
ALU = mybir.AluOpType
AF = mybir.ActivationFunctionType

D = 2048
D_A = 1024
D_IN = 5120
EPS = 1e-6
TWO_PI = 2.0 * math.pi
NCORES = 8
PIPELINE = True


class Sched:
    ENG = ("pe", "act", "dve", "pool", "sp")

    def __init__(self, sems, dma_sems):
        self.streams = {e: [] for e in self.ENG}
        self.sem = sems
        self.tick = {e: 0 for e in self.ENG}
        self.waited = {e: {} for e in self.ENG}
        self.last_w = {}
        self.readers = {}
        self.dma_sems = dma_sems
        self.dma_i = {q: 0 for q in dma_sems}
        self.out_events = []

    def _need(self, eng, ev):
        if ev is None:
            return
        sem, val, src = ev
        if src == eng and eng == "pe":
            return
        key = id(sem)
        if self.waited[eng].get(key, 0) >= val:
            return
        self.waited[eng][key] = val
        self.streams[eng].append(("wait", sem, val))

    def _deps(self, eng, reads, writes):
        for k in reads:
            self._need(eng, self.last_w.get(k))
        for k in writes:
            self._need(eng, self.last_w.get(k))
            for ev in self.readers.get(k, []):
                self._need(eng, ev)

    def _commit(self, ev, reads, writes):
        for k in reads:
            self.readers.setdefault(k, []).append(ev)
        for k in writes:
            self.last_w[k] = ev
            self.readers[k] = []

    def op(self, eng, fn, reads=(), writes=()):
        self._deps(eng, reads, writes)
        self.tick[eng] += 1
        ev = (self.sem[eng], self.tick[eng], eng)
        self.streams[eng].append(("op", fn, self.sem[eng], 1))
        self._commit(ev, reads, writes)
        return ev

    def pe(self, fns, reads=(), writes=()):
        self._deps("pe", reads, writes)
        for fn in fns[:-1]:
            self.streams["pe"].append(("op", fn, None, 0))
        self.tick["pe"] += 1
        ev = (self.sem["pe"], self.tick["pe"], "pe")
        self.streams["pe"].append(("op", fns[-1], self.sem["pe"], 1))
        self._commit(ev, reads, writes)
        return ev

    def dma(self, q, fn, reads=(), writes=(), is_out=False):
        ring = self.dma_sems[q]
        i = self.dma_i[q]
        self.dma_i[q] += 1
        sem = ring[i % len(ring)]
        gen = i // len(ring)
        if gen > 0:
            self._need(q, (sem, 16 * gen, "dma"))
        self._deps(q, reads, writes)
        ev = (sem, 16 * (gen + 1), "dma")
        self.streams[q].append(("op", fn, sem, 16))
        self._commit(ev, reads, writes)
        if is_out:
            self.out_events.append(ev)
        return ev

    def barrier_all(self):
        engs = ("pe", "act", "dve", "pool", "sp")
        evs = [(self.sem[e], self.tick[e], e) for e in ("pe", "act", "dve", "pool") if self.tick[e] > 0]
        for q, ring in self.dma_sems.items():
            n = self.dma_i[q]
            for j, sem in enumerate(ring):
                cnt = (n - j + len(ring) - 1) // len(ring) if n > j else 0
                if cnt > 0:
                    evs.append((sem, 16 * cnt, "dma"))
        for e in engs:
            for ev in evs:
                if not (ev[2] == e and e == "pe"):
                    self._need(e, ev)
        self.last_w = {}
        self.readers = {}

    def finish(self):
        for ev in self.out_events:
            self._need("sp", ev)

    def emit(self, block):
        def run(engine, name):
            for item in self.streams[name]:
                if item[0] == "wait":
                    engine.wait_ge(item[1], item[2])
                else:
                    _, fn, sem, inc = item
                    ins = fn(engine)
                    if sem is not None:
                        ins.then_inc(sem, inc)

        @block.tensor
        def _(e):
            run(e, "pe")

        @block.scalar
        def _(e):
            run(e, "act")

        @block.vector
        def _(e):
            run(e, "dve")

        @block.gpsimd
        def _(e):
            run(e, "pool")

        @block.sync
        def _(e):
            run(e, "sp")


class Arena:
    def __init__(self, ar, nbytes):
        self.ar = ar
        self.n = nbytes
        self.off = 0

    def alloc(self, nbytes):
        o = self.off
        self.off += (nbytes + 63) // 64 * 64
        assert self.off <= self.n, f"arena overflow {self.off} > {self.n}"
        return o

    def view(self, off, shape, dt, parts=128, p0=0):
        esz = 2 if dt == BF16 else 4
        nelem = int(np.prod(shape))
        w0 = off // 4
        nw = nelem * esz // 4
        a = self.ar[p0:p0 + parts, w0:w0 + nw]
        if dt != F32:
            a = a.bitcast(dt)
        if len(shape) == 2:
            a = a.rearrange("p (a b) -> p a b", a=shape[0])
        elif len(shape) == 3:
            a = a.rearrange("p (a b c) -> p a b c", a=shape[0], b=shape[1])
        return a

    def new(self, shape, dt, parts=128):
        esz = 2 if dt == BF16 else 4
        off = self.alloc(int(np.prod(shape)) * esz)
        return self.view(off, shape, dt, parts), off


def build_nc():
    nc = bass.Bass("TRN2", target_bir_lowering=False)

    def din(name, shape):
        return nc.dram_tensor(name, list(shape), F32, kind="ExternalInput").ap()

    def dout(name, shape):
        return nc.dram_tensor(name, list(shape), F32, kind="ExternalOutput").ap()

    xo = din("xo", (1024, D)); xp = din("xp", (1024, D)); xs = din("xs", (16, D))
    cs_d = din("cs", (33, D)); sre_d = din("sre", (16, 4096)); sim_d = din("sim", (16, 4096))
    maskv_d = din("maskv", (128, 1))
    w_c = din("w_c", (D, 3 * D)); b_c = din("b_c", (1, 3 * D))
    gpre_d = din("gpre_fm", (128, 16)); gpost_d = din("gpost_fm", (128, 16))
    w_in = din("w_in", (D, D_IN)); w_out = din("w_out", (D, D))
    lng_d = din("lng", (1, 1024)); lnb_d = din("lnb", (1, 1024))
    wst_d = din("ws_t", (128, 1024)); bsrow_d = din("bsrow", (1, 1024))
    w00_d = din("w00", (1, 8)); bs0_d = din("bs0", (1, 8))
    are_d = din("a_re2", (128, 32)); aim_d = din("a_im2", (128, 32)); ldt_d = din("ldt2", (128, 32))
    Bre_d = din("Bre_pad", (128, 4096)); Bim_d = din("Bim_pad", (128, 4096))
    Cre_d = din("Cre_pad", (128, 4096)); Cim_d = din("Cim_pad", (128, 4096))
    dfm_d = din("dfm", (128, 8)); wgv_d = din("wgv_pad", (128, 1024)); wgg_d = din("wgg_pad", (128, 1024))
    bgv_d = din("bgv", (128, 8)); bgg_d = din("bgg", (128, 8))

    yo = dout("yo", (1024, D)); ys = dout("ys", (16, D)); vs_o = dout("vs", (16, 1024))
    hpre_o = dout("hp_re", (32, 128)); hpim_o = dout("hp_im", (32, 128))
    hsre_o = dout("hs_re", (16, 4096)); hsim_o = dout("hs_im", (16, 4096))

    w_in_v = w_in.rearrange("(kc p) n -> p kc n", p=128)
    w_out_v = w_out.rearrange("(kc p) n -> p kc n", p=128)
    w_c_v = w_c.rearrange("(kc p) n -> p kc n", p=128)

    ARENA_BYTES = 208896
    with ExitStack() as es:
        ar_t = es.enter_context(nc.sbuf_tensor("arena", [128, ARENA_BYTES // 4], F32))
        A = Arena(ar_t, ARENA_BYTES)

        def ps(name, shape, dt):
            return es.enter_context(nc.psum_tensor(name, shape, dt))

        def sem(name):
            return es.enter_context(nc.semaphore(name))

        pm = [ps("pm0", [128, 512], F32), ps("pm1", [128, 512], F32)]
        pbr = ps("pbr", [128, 4, 128], F32)
        pbi = ps("pbi", [128, 4, 128], F32)
        pyg = ps("pyg", [128, 8, 128], F32)
        ptb = ps("ptb", [128, 4, 128], BF16)
        px = ps("px", [128, 512], F32)
        pssq = pbr[:].rearrange("p a b -> p (a b)")

        sems = {e: sem("s_" + e) for e in Sched.ENG}
        dsems = {q: [sem(f"d_{q}{i}") for i in range(8)] for q in ("sp", "pool", "act")}
        block = es.enter_context(nc.Block())
        S = Sched(sems, dsems)

        hT, hT_off = A.new((16, 512), BF16)
        yT, yT_off = A.new((16, 512), BF16)
        vraw, vraw_off = A.new((4, 1024), F32)
        Ct, Ct_off = A.new((32, 128), F32)
        St, St_off = A.new((32, 128), F32)
        Bre, _ = A.new((32, 128), BF16)
        Bim, _ = A.new((32, 128), BF16)
        Cre, Cre_off = A.new((32, 128), BF16)
        nCim, nCim_off = A.new((32, 128), BF16)
        Wgv, _ = A.new((8, 128), BF16)
        Wgg, _ = A.new((8, 128), BF16)
        NWB = 3
        wblk_alloc = [A.new((16, 128), BF16) for _ in range(NWB)]
        wblk = [v for v, _ in wblk_alloc]
        wblk_off = wblk_alloc[0][1]
        xbT, _ = A.new((512,), BF16)
        szb, _ = A.new((512,), F32)
        u_sb, u_off = A.new((512,), F32)
        sza, sza_off = A.new((512,), F32)
        xbT2 = [xbT, A.view(sza_off, (512,), BF16)]
        szb2 = [szb, A.view(u_off, (512,), F32)]
        s5w = [A.new((4, 128), F32) for _ in range(6)]
        tA, tB, tC, tD, gr, gi = [v for v, _ in s5w]
        s5w_off = s5w[0][1]
        hrb2 = [A.new((4, 128), BF16)[0] for _ in range(2)]
        hib2 = [A.new((4, 128), BF16)[0] for _ in range(2)]
        ybf2 = [A.new((128,), BF16)[0] for _ in range(2)]
        sig2 = [A.new((128,), F32)[0] for _ in range(2)]
        hrb, hib = hrb2[0], hib2[0]
        tmp128, _ = A.new((128,), F32)
        xt, xt_off = A.new((D,), F32)
        outt, outt_off = A.new((D,), F32)
        xn = None
        Wt, _ = A.new((8, 128), F32)
        Bsbc, _ = A.new((8, 128), F32)
        lng, _ = A.new((1024,), F32)
        lnb, _ = A.new((1024,), F32)
        Afm, _ = A.new((16, 33), F32)
        Shfm, _ = A.new((16, 33), F32)
        GGfm, _ = A.new((16, 33), F32)
        gpre, _ = A.new((16,), F32)
        gpost, _ = A.new((16,), F32)
        identf, _ = A.new((128,), F32)
        identb, _ = A.new((128,), BF16)
        ones_f, _ = A.new((128,), F32)
        small = {}
        for nm in ("are", "aim", "dt", "rho", "th", "abr", "abi", "cfr", "cfi", "icr", "ici", "e1r", "e1i",
                   "eTr", "eTi", "ginr", "gini", "p1", "p2", "p3", "p4", "p5", "p6"):
            small[nm], _ = A.new((32,), F32)
        small_i, _ = A.new((32,), I32)
        A128r, _ = A.new((32,), F32)
        A128i, _ = A.new((32,), F32)
        gin2r, _ = A.new((8, 16), F32)
        gin2i, _ = A.new((8, 16), F32)
        dfm, _ = A.new((8,), F32); bgv, _ = A.new((8,), F32); bgg, _ = A.new((8,), F32)
        w00bc, _ = A.new((8,), F32); bs0bc, _ = A.new((8,), F32)
        Wsamp, _ = A.new((8, 16), F32)
        maskv, _ = A.new((1,), F32)
        st4, _ = A.new((8,), F32)
        rstd_bc, rstd_off = A.new((512,), F32)
        zt, zt_off = A.new((512,), F32)
        sqt = A.view(zt_off, (512,), F32)
        assert zt_off == rstd_off + 2048
        xn = A.view(rstd_off, (D,), BF16)
        tauf, _ = A.new((128,), F32)
        taui, _ = A.new((128,), I32)
        hTs, _ = A.new((16, 16), BF16)
        yTs, _ = A.new((16, 16), BF16)
        print("arena used", A.off)

        oT_lo = A.view(hT_off, (8, 512), F32)
        oT_hi = A.view(vraw_off, (8, 512), F32)
        oTs = A.view(u_off, (16, 16), F32)
        wblk_samp = wblk + [A.view(hT_off + i * 4096, (16, 128), BF16) for i in range(4)] \
                         + [A.view(yT_off + i * 4096, (16, 128), BF16) for i in range(4)]
        cur = {"yT": yT}
        RCt = A.view(yT_off, (32, 128), F32)
        RSt = A.view(vraw_off, (32, 128), F32)
        accs = A.view(s5w[0][1], (4, 32, 4), F32)
        junkR = A.view(s5w[1][1], (128,), F32)
        tauR = A.view(xt_off, (128,), F32)
        wcb = [A.view(xt_off, (8, 512), BF16), A.view(xt_off + 8192, (8, 512), BF16), A.view(Cre_off, (8, 512), BF16)]
        m3 = A.view(nCim_off, (D,), F32, parts=33)
        cs_t = A.view(wblk_off, (D,), F32, parts=33)
        bcb = A.view(s5w_off, (D,), F32, parts=33)
        csT = A.view(s5w_off + 8192, (16, 33), BF16)
        targ = A.view(hT_off, (32, 128), F32)
        tq = A.view(yT_off, (32, 128), F32)
        tqi = A.view(vraw_off, (32, 128), I32)
        cstage = A.view(vraw_off, (32, 128), F32)
        cstage2 = A.view(hT_off, (32, 128), F32)
        sre_t = A.view(vraw_off, (4096,), F32, parts=16)
        sim_t = A.view(xt_off, (4096,), F32, parts=16)
        targ_alt = A.view(xt_off, (32, 128), F32)
        rhoZ = A.view(vraw_off, (32, 128), F32)

        defer = {"on": False, "ops": []}

        def dve(fn, r=(), w=()):
            if defer["on"]:
                defer["ops"].append(("dve", fn, r, w))
                return None
            return S.op("dve", fn, r, w)

        def act(fn, r=(), w=()):
            if defer["on"]:
                defer["ops"].append(("act", fn, r, w))
                return None
            return S.op("act", fn, r, w)

        def op_cost(w):
            return 40 if any(k.startswith("targ") or k in ("St", "Ct") for k in w) else 1

        def emit_deferred(budget):
            spent = 0
            while defer["ops"] and spent < budget:
                eng, fn, r, w = defer["ops"].pop(0)
                S.op(eng, fn, r, w)
                spent += op_cost(w)

        def pool(fn, r=(), w=()):
            return S.op("pool", fn, r, w)

        def range_reduce(arg, q, qi, shape_key):
            dve(lambda e: e.tensor_scalar(out=q, in0=arg, scalar1=1.0 / TWO_PI, scalar2=None, op0=ALU.mult),
                [shape_key], [shape_key + "q"])
            dve(lambda e: e.tensor_copy(out=qi, in_=q), [shape_key + "q"], [shape_key + "qi"])
            dve(lambda e: e.tensor_copy(out=q, in_=qi), [shape_key + "qi"], [shape_key + "q"])
            dve(lambda e: e.scalar_tensor_tensor(out=arg, in0=q, scalar=-TWO_PI, in1=arg, op0=ALU.mult, op1=ALU.add),
                [shape_key + "q", shape_key], [shape_key])
            dve(lambda e: e.tensor_scalar(out=q, in0=arg, scalar1=math.pi, scalar2=-TWO_PI, op0=ALU.is_gt, op1=ALU.mult),
                [shape_key], [shape_key + "q"])
            dve(lambda e: e.tensor_tensor(out=arg, in0=arg, in1=q, op=ALU.add), [shape_key, shape_key + "q"], [shape_key])
            dve(lambda e: e.tensor_scalar(out=q, in0=arg, scalar1=-math.pi, scalar2=TWO_PI, op0=ALU.is_lt, op1=ALU.mult),
                [shape_key], [shape_key + "q"])
            dve(lambda e: e.tensor_tensor(out=arg, in0=arg, in1=q, op=ALU.add), [shape_key, shape_key + "q"], [shape_key])
            dve(lambda e: e.tensor_scalar(out=arg, in0=arg, scalar1=math.pi, scalar2=-math.pi, op0=ALU.min, op1=ALU.max),
                [shape_key], [shape_key])

        pool(lambda e: e.memset(identf, 0.0), [], ["identf"])
        pool(lambda e: e.affine_select(out=identf, in_=identf, compare_op=ALU.not_equal, fill=1.0, base=0,
                                       pattern=[[-1, 128]], channel_multiplier=1), ["identf"], ["identf"])
        dve(lambda e: e.tensor_copy(out=identb, in_=identf), ["identf"], ["identb"])
        dve(lambda e: e.memset(ones_f, 1.0), [], ["ones_f"])
        pool(lambda e: e.iota(taui, pattern=[[1, 128]], base=0, channel_multiplier=0), [], ["taui"])
        dve(lambda e: e.tensor_copy(out=tauf, in_=taui), ["taui"], ["tauf"])

        def ld(dst, src, key, q="sp"):
            S.dma(q, lambda e: e.dma_start(out=dst, in_=src), [], [key])

        ld(gpre, gpre_d, "gpre"); ld(gpost, gpost_d, "gpost")
        ld(small["are"], are_d, "are"); ld(small["aim"], aim_d, "aim"); ld(small["dt"], ldt_d, "dt")
        ld(dfm, dfm_d, "dfm"); ld(bgv, bgv_d, "bgv"); ld(bgg, bgg_d, "bgg"); ld(maskv, maskv_d, "maskv")
        ld(w00bc[0:16], w00_d.partition_broadcast(16), "w00bc")
        ld(bs0bc, bs0_d.partition_broadcast(128), "bs0bc")
        ld(lng, lng_d.partition_broadcast(128), "lng"); ld(lnb, lnb_d.partition_broadcast(128), "lnb")
        ld(Bsbc.rearrange("p a b -> p (a b)"), bsrow_d.partition_broadcast(128), "Bsbc")
        ld(Wt.rearrange("p a b -> p (a b)"), wst_d, "Wt")
        ld(cs_t, cs_d, "cs")
        S.dma("pool", lambda e: e.dma_start(out=Bre.rearrange("p a b -> p (a b)"), in_=Bre_d), [], ["Bre"])
        S.dma("pool", lambda e: e.dma_start(out=Bim.rearrange("p a b -> p (a b)"), in_=Bim_d), [], ["Bim"])
        S.dma("pool", lambda e: e.dma_start(out=Wgv.rearrange("p a b -> p (a b)"), in_=wgv_d), [], ["Wgv"])
        S.dma("pool", lambda e: e.dma_start(out=Wgg.rearrange("p a b -> p (a b)"), in_=wgg_d), [], ["Wgg"])
        for h in range(8):
            pool(lambda e, h=h: e.affine_select(out=Wt[:, h, :], in_=Wt[:, h, :], compare_op=ALU.is_ge, fill=0.0,
                                                base=0, pattern=[[1, 128]], channel_multiplier=-1), ["Wt"], ["Wt"])
        for h in range(8):
            dve(lambda e, h=h: e.tensor_scalar(out=Wsamp[0:16, h, :], in0=identf[0:16, 0:16], scalar1=w00bc[0:16, h:h + 1],
                                               scalar2=None, op0=ALU.mult), ["identf", "w00bc"], ["Wsamp"])

        defer["on"] = True
        sm = small
        act(lambda e: e.activation(out=sm["dt"], in_=sm["dt"], func=AF.Exp), ["dt"], ["dt"])
        dve(lambda e: e.tensor_tensor(out=sm["p1"], in0=sm["dt"], in1=sm["are"], op=ALU.mult), ["dt", "are"], ["p1"])
        act(lambda e: e.activation(out=sm["rho"], in_=sm["p1"], func=AF.Exp), ["p1"], ["rho"])
        dve(lambda e: e.tensor_tensor(out=sm["th"], in0=sm["dt"], in1=sm["aim"], op=ALU.mult), ["dt", "aim"], ["th"])

        def sincos(arg_src, mult, out_s, out_c, tagp):
            dve(lambda e: e.tensor_scalar(out=sm["p2"], in0=arg_src, scalar1=float(mult), scalar2=None, op0=ALU.mult),
                ["th"], ["p2"])
            range_reduce(sm["p2"], sm["p3"], small_i, "p2")
            act(lambda e: e.activation(out=out_s, in_=sm["p2"], func=AF.Sin), ["p2"], [tagp + "s"])
            dve(lambda e: e.tensor_scalar(out=sm["p2"], in0=arg_src, scalar1=float(mult), scalar2=math.pi / 2,
                                          op0=ALU.mult, op1=ALU.add), ["th", tagp + "s"], ["p2"])
            range_reduce(sm["p2"], sm["p3"], small_i, "p2")
            act(lambda e: e.activation(out=out_c, in_=sm["p2"], func=AF.Sin), ["p2"], [tagp + "c"])

        sincos(sm["th"], 1.0, sm["e1i"], sm["e1r"], "e1")
        sincos(sm["th"], 128.0, sm["eTi"], sm["eTr"], "eT")
        K1 = ["e1s", "e1c", "rho"]
        dve(lambda e: e.tensor_tensor(out=sm["abr"], in0=sm["rho"], in1=sm["e1r"], op=ALU.mult), K1, ["abr"])
        dve(lambda e: e.tensor_tensor(out=sm["abi"], in0=sm["rho"], in1=sm["e1i"], op=ALU.mult), K1, ["abi"])
        dve(lambda e: e.tensor_scalar(out=sm["p1"], in0=sm["abr"], scalar1=-1.0, scalar2=None, op0=ALU.add), ["abr"], ["p1"])
        dve(lambda e: e.tensor_tensor(out=sm["p4"], in0=sm["are"], in1=sm["are"], op=ALU.mult), ["are"], ["p4"])
        dve(lambda e: e.tensor_tensor(out=sm["p5"], in0=sm["aim"], in1=sm["aim"], op=ALU.mult), ["aim"], ["p5"])
        dve(lambda e: e.tensor_tensor(out=sm["p4"], in0=sm["p4"], in1=sm["p5"], op=ALU.add), ["p4", "p5"], ["p4"])
        dve(lambda e: e.reciprocal(out=sm["p4"], in_=sm["p4"]), ["p4"], ["p4"])
        dve(lambda e: e.tensor_tensor(out=sm["p5"], in0=sm["p1"], in1=sm["are"], op=ALU.mult), ["p1", "are"], ["p5"])
        dve(lambda e: e.tensor_tensor(out=sm["p6"], in0=sm["abi"], in1=sm["aim"], op=ALU.mult), ["abi", "aim"], ["p6"])
        dve(lambda e: e.tensor_tensor(out=sm["p5"], in0=sm["p5"], in1=sm["p6"], op=ALU.add), ["p5", "p6"], ["p5"])
        dve(lambda e: e.tensor_tensor(out=sm["cfr"], in0=sm["p5"], in1=sm["p4"], op=ALU.mult), ["p5", "p4"], ["cfr"])
        dve(lambda e: e.tensor_tensor(out=sm["p5"], in0=sm["abi"], in1=sm["are"], op=ALU.mult), ["abi", "are", "cfr"], ["p5"])
        dve(lambda e: e.tensor_tensor(out=sm["p6"], in0=sm["p1"], in1=sm["aim"], op=ALU.mult), ["p1", "aim"], ["p6"])
        dve(lambda e: e.tensor_tensor(out=sm["p5"], in0=sm["p5"], in1=sm["p6"], op=ALU.subtract), ["p5", "p6"], ["p5"])
        dve(lambda e: e.tensor_tensor(out=sm["cfi"], in0=sm["p5"], in1=sm["p4"], op=ALU.mult), ["p5", "p4"], ["cfi"])
        dve(lambda e: e.tensor_tensor(out=sm["p4"], in0=sm["cfr"], in1=sm["cfr"], op=ALU.mult), ["cfr", "cfi"], ["p4"])
        dve(lambda e: e.tensor_tensor(out=sm["p5"], in0=sm["cfi"], in1=sm["cfi"], op=ALU.mult), ["cfi", "p4"], ["p5"])
        dve(lambda e: e.tensor_tensor(out=sm["p4"], in0=sm["p4"], in1=sm["p5"], op=ALU.add), ["p4", "p5"], ["p4"])
        dve(lambda e: e.reciprocal(out=sm["p4"], in_=sm["p4"]), ["p4"], ["p4"])
        dve(lambda e: e.tensor_tensor(out=sm["icr"], in0=sm["cfr"], in1=sm["p4"], op=ALU.mult), ["cfr", "p4"], ["icr"])
        dve(lambda e: e.scalar_tensor_tensor(out=sm["ici"], in0=sm["cfi"], scalar=-1.0, in1=sm["p4"], op0=ALU.mult, op1=ALU.mult),
            ["cfi", "p4"], ["ici"])
        dve(lambda e: e.memset(gin2r, 0.0), [], ["gin"])
        dve(lambda e: e.memset(gin2i, 0.0), ["gin"], ["gin"])

        th_b = sm["th"].unsqueeze(2).to_broadcast([128, 32, 128])
        tau_b = tauf.unsqueeze(1).to_broadcast([128, 32, 128])
        dve(lambda e: e.tensor_tensor(out=targ, in0=th_b, in1=tau_b, op=ALU.mult), ["th", "tauf"], ["targ"])
        range_reduce(targ, tq, tqi, "targ")
        act(lambda e: e.activation(out=St, in_=targ, func=AF.Sin), ["targ"], ["St"])
        dve(lambda e: e.tensor_tensor(out=targ, in0=th_b, in1=tau_b, op=ALU.mult), ["th", "tauf", "St"], ["targ"])
        dve(lambda e: e.tensor_scalar(out=targ, in0=targ, scalar1=math.pi / 2, scalar2=None, op0=ALU.add), ["targ"], ["targ"])
        range_reduce(targ, tq, tqi, "targ")
        act(lambda e: e.activation(out=Ct, in_=targ, func=AF.Sin), ["targ"], ["Ct"])
        defer["on"] = False
        per_step = (sum(op_cost(o[3]) for o in defer["ops"]) + 10) // 11
        act(lambda e: e.activation(out=cs_t, in_=cs_t, func=AF.Silu), ["cs"], ["cs"])
        for kc in range(16):
            S.pe([lambda e, kc=kc: e.transpose(px[:, 0:33], cs_t[:, kc * 128:(kc + 1) * 128], identf[0:33, 0:33])],
                 ["cs", "identf"], ["px"])
            dve(lambda e, kc=kc: e.tensor_copy(out=csT[:, kc, :], in_=px[:, 0:33]), ["px"], ["csT"])
        wci = 0
        for third, dstT in enumerate((Shfm, Afm, GGfm)):
            S.dma("sp", lambda e, third=third: e.dma_start(out=bcb, in_=b_c[:, third * D:(third + 1) * D].partition_broadcast(33)),
                  [], ["bcb"])
            for nb in range(4):
                col0 = third * D + nb * 512
                fns = []
                for kg in range(2):
                    buf = wcb[wci % 3]
                    key = f"wcb{wci % 3}"
                    wci += 1
                    S.dma("pool",
                          lambda e, buf=buf, kg=kg, col0=col0: e.dma_start(out=buf, in_=w_c_v[:, kg * 8:(kg + 1) * 8, col0:col0 + 512]),
                          [], [key])
                    S.pe([lambda e, buf=buf, kg=kg, i=i, nb=nb: e.matmul(pm[nb % 2][0:33, :], lhsT=csT[:, kg * 8 + i, :], rhs=buf[:, i, :],
                                                                        start=(kg == 0 and i == 0), stop=(kg == 1 and i == 7))
                          for i in range(8)], [key, "csT"], [f"pm{nb % 2}"])
                dve(lambda e, nb=nb: e.tensor_tensor(out=m3[:, nb * 512:(nb + 1) * 512], in0=pm[nb % 2][0:33, :],
                                                     in1=bcb[:, nb * 512:(nb + 1) * 512], op=ALU.add), [f"pm{nb % 2}", "bcb"], ["m3"])
                emit_deferred(per_step)
            for kc in range(16):
                S.pe([lambda e, kc=kc: e.transpose(px[:, 0:33], m3[:, kc * 128:(kc + 1) * 128], identf[0:33, 0:33])],
                     ["m3", "identf"], ["px"])
                dve(lambda e, kc=kc, dstT=dstT: e.tensor_copy(out=dstT[:, kc, :], in_=px[:, 0:33]), ["px"], ["ada"])
        dve(lambda e: e.tensor_scalar(out=Afm, in0=Afm, scalar1=1.0, scalar2=None, op0=ALU.add), ["ada"], ["ada"])
        dve(lambda e: e.tensor_tensor(out=Afm, in0=Afm, in1=gpre.unsqueeze(2).to_broadcast([128, 16, 33]), op=ALU.mult),
            ["ada", "gpre"], ["ada"])
        dve(lambda e: e.tensor_tensor(out=GGfm, in0=GGfm, in1=gpost.unsqueeze(2).to_broadcast([128, 16, 33]), op=ALU.mult),
            ["ada", "gpost"], ["ada"])
        emit_deferred(10 ** 9)
        S.barrier_all()
        S.dma("sp", lambda e: e.dma_start(out=cstage.rearrange("p a b -> p (a b)"), in_=Cre_d), [], ["cst"])
        S.dma("act", lambda e: e.dma_start(out=cstage2.rearrange("p a b -> p (a b)"), in_=Cim_d), [], ["cst2"])
        cf_r = sm["cfr"].unsqueeze(2).to_broadcast([128, 32, 128])
        cf_i = sm["cfi"].unsqueeze(2).to_broadcast([128, 32, 128])
        dve(lambda e: e.tensor_tensor(out=tq, in0=cstage, in1=cf_r, op=ALU.mult), ["cst", "cfr"], ["tq"])
        dve(lambda e: e.tensor_tensor(out=targ_alt, in0=cstage2, in1=cf_i, op=ALU.mult), ["cst2", "cfi"], ["ta"])
        dve(lambda e: e.tensor_tensor(out=Cre, in0=tq, in1=targ_alt, op=ALU.subtract), ["tq", "ta"], ["Cre"])
        dve(lambda e: e.tensor_tensor(out=tq, in0=cstage, in1=cf_i, op=ALU.mult), ["cst", "cfi", "Cre"], ["tq"])
        dve(lambda e: e.tensor_tensor(out=targ_alt, in0=cstage2, in1=cf_r, op=ALU.mult), ["cst2", "cfr", "Cre"], ["ta"])
        dve(lambda e: e.tensor_tensor(out=tq, in0=tq, in1=targ_alt, op=ALU.add), ["tq", "ta"], ["tq"])
        dve(lambda e: e.tensor_scalar(out=nCim, in0=tq, scalar1=-1.0, scalar2=None, op0=ALU.mult), ["tq"], ["nCim"])
        S.barrier_all()

        wb_state = {"i": 0, "pm": 0}
        s5v = lambda i, shape: A.view(s5w[i][1], shape, F32)
        h0r_v = s5v(4, (32, 16)); h0i_v = s5v(5, (32, 16))
        tA_v = s5v(0, (32, 16)); tB_v = s5v(1, (32, 16)); tC_v = s5v(2, (32, 16)); tD_v = s5v(3, (32, 16))

        def bc3(small_ap, n):
            return small_ap.unsqueeze(2).to_broadcast([128, 32, n])

        def run_pass(kind, x_src, tok0, out_dst):
            ntok = 16 if kind == "samp" else 512
            ntile = 1 if kind == "samp" else 4
            rows = 16 if kind == "samp" else 128
            samp = kind == "samp"
            hTp = hTs if samp else hT
            yTp = yTs if samp else yT
            cur["yT"] = yTp
            ring = wblk_samp if samp else wblk
            depth = len(ring) - 1
            oTsel = (lambda f: oTs[:, f, :]) if samp else (lambda f: oT_lo[:, f, :] if f < 8 else oT_hi[:, f - 8, :])
            xbufs = [(xt, "xt"), (outt, "outt")]
            v4 = lambda t_: t_[:].rearrange("p (a b) -> p a b", a=4)
            tbanks = [(v4(pm[0]), "pm0"), (v4(pm[1]), "pm1"), (v4(px), "px"), (pyg[:, 0:4, :], "pygb0")]

            def load_x(t):
                buf, key = xbufs[t % 2]
                r0 = tok0 + t * 128
                S.dma("sp", lambda e, r0=r0, buf=buf: e.dma_start(out=buf[0:rows], in_=x_src[r0:r0 + rows, :]), [], [key])

            def phaseA(t):
                xb_, xk_ = xbufs[t % 2]
                o = 4 * (t % 2)
                sk = f"st4_{t % 2}"
                dve(lambda e: e.memset(st4[:, o:o + 4], 0.0), [], [sk])
                act(lambda e: e.activation(out=xn[0:rows], in_=xb_[0:rows], func=AF.Square, accum_out=st4[0:rows, o:o + 1]),
                    [xk_, sk], ["xn", sk, "rstd", "zt"])
                act(lambda e: e.activation(out=st4[0:rows, o + 1:o + 2], in_=st4[0:rows, o:o + 1], func=AF.Sqrt, bias=EPS, scale=1.0 / D),
                    [sk], [sk])
                dve(lambda e: e.reciprocal(out=st4[0:rows, o + 2:o + 3], in_=st4[0:rows, o + 1:o + 2]), [sk], [sk])
                act(lambda e: e.activation(out=xb_[0:rows], in_=xb_[0:rows], func=AF.Copy, scale=st4[0:rows, o + 2:o + 3]),
                    [xk_, sk], [xk_])

            def phaseB(t):
                xb_, xk_ = xbufs[t % 2]
                for g4 in range(4):
                    tb, tkey = tbanks[g4]
                    S.pe([lambda e, kc=g4 * 4 + i, i=i, tb=tb, xb_=xb_: e.transpose(tb[:, i, 0:rows], xb_[0:rows, kc * 128:(kc + 1) * 128],
                                                                                   identf[0:rows, 0:rows]) for i in range(4)],
                         [xk_, "identf"], [tkey])
                    for i in range(4):
                        kc = g4 * 4 + i
                        if not samp:
                            if i % 2 == 0:
                                act(lambda e, kc=kc, i=i, t=t, tb=tb: e.activation(out=hTp[:, kc, t * 128:(t + 1) * 128], in_=tb[:, i, :],
                                                                                   func=AF.Identity, scale=Afm[:, kc, 32:33],
                                                                                   bias=Shfm[:, kc, 32:33]), [tkey, "ada"], ["hT"])
                            else:
                                dve(lambda e, kc=kc, i=i, t=t, tb=tb: e.tensor_scalar(out=hTp[:, kc, t * 128:(t + 1) * 128], in0=tb[:, i, :],
                                                                                      scalar1=Afm[:, kc, 32:33], scalar2=Shfm[:, kc, 32:33],
                                                                                      op0=ALU.mult, op1=ALU.add), [tkey, "ada"], ["hT"])
                        else:
                            dve(lambda e, kc=kc, i=i, tb=tb: e.tensor_tensor(out=tmp128[:, 0:16], in0=tb[:, i, 0:16],
                                                                             in1=Afm[:, kc, 0:16], op=ALU.mult), [tkey, "ada"], ["tmp128"])
                            dve(lambda e, kc=kc: e.tensor_tensor(out=hTp[:, kc, 0:16], in0=tmp128[:, 0:16],
                                                                 in1=Shfm[:, kc, 0:16], op=ALU.add), ["tmp128", "ada"], ["hT"])

            load_x(0)
            if ntile > 1:
                load_x(1)
            phaseA(0)
            for t in range(ntile):
                if t + 1 < ntile:
                    phaseA(t + 1)
                phaseB(t)
                if t + 2 < ntile:
                    load_x(t + 2)
            blocks = []
            if kind != "prev":
                for b in range(8):
                    blocks.append(("v", w_in_v, 1024 + b * 128, b))
                for h in range(8):
                    blocks.append(("u", w_in_v, h * 128, h))
                    blocks.append(("za", w_in_v, 2048 + h * 128, h))
            for j in range(8):
                blocks.append(("xb", w_in_v, 3072 + j * 128, j))
                if kind != "prev":
                    blocks.append(("zb", w_in_v, 4096 + j * 128, j))
            if kind != "prev":
                for f in range(16):
                    blocks.append(("o", w_out_v, f * 128, f))
            loaded = {}

            def issue(bi):
                if bi >= len(blocks) or bi in loaded:
                    return
                _, src, col0, _ = blocks[bi]
                i = bi
                buf = ring[i % len(ring)]; key = f"wblk{i % len(ring)}"
                S.dma("pool", lambda e, buf=buf, src=src, col0=col0: e.dma_start(out=buf, in_=src[:, :, col0:col0 + 128]),
                      [], [key])
                loaded[bi] = (buf, key)

            for b0 in range(depth):
                issue(b0)
            for bi, (bk, src, col0, idx) in enumerate(blocks):
                issue(bi + depth)
                buf, wkey = loaded[bi]
                pmi = wb_state["pm"] % 2; wb_state["pm"] += 1
                pmb = pm[pmi]; pkey = f"pm{pmi}"
                if bk == "v":
                    for t in range(ntile):
                        S.pe([lambda e, kc=kc, t=t, buf=buf, pmb=pmb: e.matmul(pmb[0:rows, t * 128:(t + 1) * 128],
                                                                            lhsT=hTp[:, kc, t * 128:t * 128 + rows],
                                                                            rhs=buf[:, kc, :], start=(kc == 0), stop=(kc == 15))
                              for kc in range(16)], [wkey, "hT"], [pkey])
                    act(lambda e, pmb=pmb, idx=idx: e.activation(
                        out=vraw[0:rows, 0:ntile, idx * 128:(idx + 1) * 128],
                        in_=pmb[0:rows, 0:ntile * 128].rearrange("p (a b) -> p a b", a=ntile), func=AF.Copy),
                        [pkey], ["vraw"])
                    if idx == 7:
                        for t in range(ntile):
                            vr = vraw[0:rows, t, :]
                            dve(lambda e: e.memset(st4, 0.0), [], ["st4"])
                            act(lambda e, vr=vr: e.activation(out=outt[0:rows, 0:1024], in_=vr, func=AF.Identity,
                                                              accum_out=st4[0:rows, 0:1]), ["vraw", "st4"], ["outt", "st4"])
                            act(lambda e, vr=vr: e.activation(out=outt[0:rows, 0:1024], in_=vr, func=AF.Square,
                                                              accum_out=st4[0:rows, 1:2]), ["vraw", "st4"], ["outt", "st4"])
                            dve(lambda e: e.tensor_scalar(out=st4[0:rows, 2:3], in0=st4[0:rows, 0:1], scalar1=1.0 / 1024,
                                                          scalar2=None, op0=ALU.mult), ["st4"], ["st4"])
                            dve(lambda e: e.tensor_tensor(out=st4[0:rows, 3:4], in0=st4[0:rows, 2:3], in1=st4[0:rows, 2:3],
                                                          op=ALU.mult), ["st4"], ["st4"])
                            dve(lambda e: e.scalar_tensor_tensor(out=st4[0:rows, 4:5], in0=st4[0:rows, 1:2], scalar=1.0 / 1024,
                                                                 in1=st4[0:rows, 3:4], op0=ALU.mult, op1=ALU.subtract),
                                ["st4"], ["st4"])
                            act(lambda e: e.activation(out=st4[0:rows, 5:6], in_=st4[0:rows, 4:5], func=AF.Sqrt, bias=EPS, scale=1.0),
                                ["st4"], ["st4"])
                            dve(lambda e: e.reciprocal(out=st4[0:rows, 6:7], in_=st4[0:rows, 5:6]), ["st4"], ["st4"])
                            dve(lambda e: e.scalar_tensor_tensor(out=st4[0:rows, 7:8], in0=st4[0:rows, 2:3], scalar=-1.0,
                                                                 in1=st4[0:rows, 6:7], op0=ALU.mult, op1=ALU.mult),
                                ["st4"], ["st4"])
                            act(lambda e, vr=vr: e.activation(out=vr, in_=vr, func=AF.Identity, scale=st4[0:rows, 6:7],
                                                              bias=st4[0:rows, 7:8]), ["vraw", "st4"], ["vraw"])
                            dve(lambda e, vr=vr: e.tensor_tensor(out=vr, in0=vr, in1=lng[0:rows], op=ALU.mult), ["vraw", "lng"], ["vraw"])
                            dve(lambda e, vr=vr: e.tensor_tensor(out=vr, in0=vr, in1=lnb[0:rows], op=ALU.add), ["vraw", "lnb"], ["vraw"])
                        if samp:
                            S.dma("sp", lambda e: e.dma_start(out=vs_o, in_=vraw[0:16, 0, :]), ["vraw"], [], is_out=True)
                    continue
                rhs_src = yTp if bk == "o" else hTp
                rkey = "yT" if bk == "o" else "hT"
                S.pe([lambda e, kc=kc, buf=buf, pmb=pmb, rhs_src=rhs_src: e.matmul(pmb[:, 0:ntok], lhsT=buf[:, kc, :],
                                                                                  rhs=rhs_src[:, kc, 0:ntok],
                                                                                  start=(kc == 0), stop=(kc == 15))
                      for kc in range(16)], [wkey, rkey], [pkey])
                if bk == "u":
                    act(lambda e, pmb=pmb: e.activation(out=u_sb[:, 0:ntok], in_=pmb[:, 0:ntok], func=AF.Copy), [pkey], ["u_sb"])
                elif bk == "za":
                    h = idx
                    act(lambda e, pmb=pmb: e.activation(out=sza[:, 0:ntok], in_=pmb[:, 0:ntok], func=AF.Silu), [pkey], ["sza"])
                    if not samp:
                        S.pe([lambda e, t=t, h=h: e.matmul(px[:, t * 128:(t + 1) * 128], lhsT=vraw[:, t, h * 128:(h + 1) * 128],
                                                           rhs=Wt[:, h, :], start=True, stop=True) for t in range(4)],
                             ["vraw", "Wt"], ["px"])
                        dve(lambda e, h=h: e.tensor_tensor(out=zt.rearrange("p (a b) -> p a b", a=4),
                                                           in0=px[:].rearrange("p (a b) -> p a b", a=4),
                                                           in1=Bsbc[:, h, :].unsqueeze(1).to_broadcast([128, 4, 128]), op=ALU.add),
                            ["px", "Bsbc"], ["zt"])
                    else:
                        S.pe([lambda e, h=h: e.matmul(px[:, 0:16], lhsT=vraw[0:16, 0, h * 128:(h + 1) * 128],
                                                      rhs=Wsamp[0:16, h, :], start=True, stop=True)],
                             ["vraw", "Wsamp"], ["px"])
                        dve(lambda e, h=h: e.tensor_scalar(out=zt[:, 0:16], in0=px[:, 0:16], scalar1=bs0bc[:, h:h + 1],
                                                           scalar2=None, op0=ALU.add), ["px", "bs0bc"], ["zt"])
                    dve(lambda e: e.tensor_tensor(out=zt[:, 0:ntok], in0=zt[:, 0:ntok], in1=u_sb[:, 0:ntok], op=ALU.mult),
                        ["zt", "u_sb"], ["zt"])
                    dve(lambda e, h=h: e.tensor_tensor(out=yTp[:, h, 0:ntok], in0=zt[:, 0:ntok], in1=sza[:, 0:ntok], op=ALU.mult),
                        ["zt", "sza"], ["yT"])
                elif bk == "xb":
                    act(lambda e, pmb=pmb, idx=idx: e.activation(out=xbT2[idx % 2][:, 0:ntok], in_=pmb[:, 0:ntok], func=AF.Copy),
                        [pkey], [f"xbT{idx % 2}", "u_sb", "sza"])
                    if idx == 0 and kind == "own":
                        build_rhoZ()
                    if kind == "prev":
                        if idx == 0:
                            dve(lambda e: e.memset(accs, 0.0), [], ["accs"])
                        if idx >= 1:
                            prev_unit(idx - 1)
                        if idx == 7:
                            prev_unit(7)
                            prev_combine()
                elif bk == "zb":
                    act(lambda e, pmb=pmb, idx=idx: e.activation(out=szb2[idx % 2][:, 0:ntok], in_=pmb[:, 0:ntok], func=AF.Silu),
                        [pkey], [f"szb{idx % 2}", "u_sb", "sza"])
                    if samp:
                        if idx == 0:
                            samp_prep()
                        s5_unit_samp(idx)
                        if idx == 7:
                            samp_finish()
                    else:
                        if idx >= 1:
                            s5_unit(idx - 1, True, ntok)
                            flush_pend()
                        if idx == 7:
                            s5_unit(7, True, ntok)
                            flush_pend()
                elif bk == "o":
                    f = idx
                    oTf = oTsel(f)
                    act(lambda e, pmb=pmb, oTf=oTf: e.activation(out=oTf[:, 0:ntok], in_=pmb[:, 0:ntok], func=AF.Copy), [pkey], ["oT", "vraw", "u_sb"])
                    act(lambda e, pmb=pmb: e.activation(out=sqt[:, 0:ntok], in_=pmb[:, 0:ntok], func=AF.Square), [pkey], ["sqt"])
                    S.pe([lambda e, f=f: e.matmul(pssq[:, 0:ntok], lhsT=ones_f, rhs=sqt[:, 0:ntok], start=(f == 0), stop=(f == 15))],
                         ["sqt", "ones_f"], ["pb"])
            if kind == "prev":
                S.barrier_all()
                return
            act(lambda e: e.activation(out=rstd_bc[:, 0:ntok], in_=pssq[:, 0:ntok], func=AF.Sqrt, bias=EPS, scale=1.0 / D),
                ["pb"], ["rstd"])
            dve(lambda e: e.reciprocal(out=rstd_bc[:, 0:ntok], in_=rstd_bc[:, 0:ntok]), ["rstd"], ["rstd"])
            ebufs = [(xt, "xt"), (outt, "outt")]
            zbufs = [(zt, "zt"), (sza, "sza")]
            ebanks = [(px, "px"), (pm[0], "pm0"), (pm[1], "pm1"), (pyg[:, 0:4, :].rearrange("p a b -> p (a b)"), "pygb0")]

            def load_xe(t):
                buf, key = ebufs[t % 2]
                r0 = tok0 + t * 128
                S.dma("sp", lambda e, r0=r0, buf=buf: e.dma_start(out=buf[0:rows], in_=x_src[r0:r0 + rows, :]), [], [key])

            load_xe(0)
            for t in range(ntile):
                if t + 1 < ntile:
                    load_xe(t + 1)
                r0 = tok0 + t * 128
                ebuf, ekey = ebufs[t % 2]
                for g4 in range(4):
                    zb_, zk_ = zbufs[g4 % 2]
                    pb_, pk_ = ebanks[g4]
                    for i in range(4):
                        f = g4 * 4 + i
                        oTf = oTsel(f)
                        c0 = t * 128
                        if not samp:
                            dve(lambda e, oTf=oTf, f=f, i=i, c0=c0, zb_=zb_: e.scalar_tensor_tensor(
                                out=zb_[:, i * 128:(i + 1) * 128], in0=oTf[:, c0:c0 + 128], scalar=GGfm[:, f, 32:33],
                                in1=rstd_bc[:, c0:c0 + 128], op0=ALU.mult, op1=ALU.mult), ["oT", "rstd", "ada"], [zk_])
                        else:
                            dve(lambda e, oTf=oTf, i=i, zb_=zb_: e.tensor_tensor(out=zb_[:, i * 128:i * 128 + 16], in0=oTf[:, 0:16],
                                                                                 in1=rstd_bc[:, 0:16], op=ALU.mult), ["oT", "rstd"], [zk_])
                            dve(lambda e, f=f, i=i, zb_=zb_: e.tensor_tensor(out=zb_[:, i * 128:i * 128 + 16], in0=zb_[:, i * 128:i * 128 + 16],
                                                                             in1=GGfm[:, f, 0:16], op=ALU.mult), [zk_, "ada"], [zk_])
                    S.pe([lambda e, i=i, zb_=zb_, pb_=pb_: e.transpose(pb_[0:rows, i * 128:(i + 1) * 128], zb_[:, i * 128:i * 128 + rows], identf)
                          for i in range(4)], [zk_, "identf"], [pk_])
                    dve(lambda e, g4=g4, pb_=pb_, ebuf=ebuf: e.tensor_tensor(out=ebuf[0:rows, g4 * 512:(g4 + 1) * 512], in0=pb_[0:rows, :],
                                                                            in1=ebuf[0:rows, g4 * 512:(g4 + 1) * 512], op=ALU.add),
                        [pk_, ekey], [ekey])
                S.dma("sp", lambda e, r0=r0, ebuf=ebuf: e.dma_start(out=out_dst[r0:r0 + rows, :], in_=ebuf[0:rows]), [ekey], [], is_out=True)
            S.barrier_all()

        def build_rhoZ():
            dve(lambda e: e.tensor_copy(out=rhoZ, in_=bc3(sm["rho"], 128)), ["rho"], ["vraw"])
            dve(lambda e: e.memset(rhoZ[:, :, 0:1], 0.0), ["vraw"], ["vraw"])

        def s5_stage1(j, c0, n, pp):
            xb = xbT2[j % 2]
            dve(lambda e: e.scalar_tensor_tensor(out=ybf2[pp][:, 0:n], in0=xb[:, c0:c0 + n], scalar=dfm[:, j:j + 1],
                                                 in1=pyg[:, 4 * pp, 0:n], op0=ALU.mult, op1=ALU.add),
                [f"xbT{j % 2}", "dfm", f"pygb{pp}"], [f"ybf{pp}"])
            S.pe([lambda e: e.matmul(pyg[:, 4 * pp + 1, 0:n], lhsT=Wgv[:, j, :], rhs=ybf2[pp][:, 0:n], start=True, stop=True),
                  lambda e: e.matmul(pyg[:, 4 * pp + 2, 0:n], lhsT=Wgg[:, j, :], rhs=ybf2[pp][:, 0:n], start=True, stop=True)],
                 [f"ybf{pp}", "Wgv", "Wgg"], [f"pygb{pp}"])
            act(lambda e: e.activation(out=sig2[pp][:, 0:n], in_=pyg[:, 4 * pp + 2, 0:n], func=AF.Sigmoid, bias=bgg[:, j:j + 1], scale=1.0),
                [f"pygb{pp}", "bgg"], [f"sig{pp}"])

        def s5_stage2(j, c0, n, pp):
            sz = szb2[j % 2]
            dve(lambda e: e.scalar_tensor_tensor(out=tmp128[:, 0:n], in0=pyg[:, 4 * pp + 1, 0:n], scalar=bgv[:, j:j + 1],
                                                 in1=sig2[pp][:, 0:n], op0=ALU.add, op1=ALU.mult),
                [f"pygb{pp}", "bgv", f"sig{pp}"], ["tmp128"])
            dve(lambda e, ydst=cur["yT"]: e.tensor_tensor(out=ydst[:, 8 + j, c0:c0 + n], in0=tmp128[:, 0:n], in1=sz[:, c0:c0 + n], op=ALU.mult),
                ["tmp128", f"szb{j % 2}"], ["yT"])

        P1, P2, P3, P4 = hrb2[0], hrb2[1], hib2[0], hib2[1]

        def s5_cproj(j, n, pp):
            fns = []
            for kk in range(4):
                k = 4 * j + kk
                for wi_, (wgt, src) in enumerate(((Cre, P1), (Cre, P2), (nCim, P3), (nCim, P4))):
                    fns.append(lambda e, k=k, kk=kk, wgt=wgt, src=src, first=(kk == 0 and wi_ == 0), last=(kk == 3 and wi_ == 3):
                               e.matmul(pyg[:, 4 * pp, 0:n], lhsT=wgt[:, k, :], rhs=src[:, kk, 0:n], start=first, stop=last))
            S.pe(fns, ["P", "Cre", "nCim"], [f"pygb{pp}"])

        def s5_cproj_samp(j, n):
            fns = []
            for kk in range(4):
                k = 4 * j + kk
                fns.append(lambda e, k=k, kk=kk: e.matmul(pyg[:, 0, 0:n], lhsT=Cre[:, k, :], rhs=hrb[:, kk, 0:n], start=(kk == 0), stop=False))
                fns.append(lambda e, k=k, kk=kk: e.matmul(pyg[:, 0, 0:n], lhsT=nCim[:, k, :], rhs=hib[:, kk, 0:n], start=False, stop=(kk == 3)))
            S.pe(fns, ["P", "Cre", "nCim"], ["pygb0"])

        def s5_out(j, c0, n, hr_src, hi_src):
            s5_cproj_samp(j, n)
            s5_stage1(j, c0, n, 0)
            s5_stage2(j, c0, n, 0)

        pend = []

        def advance_pend(new_item=None):
            nxt = []
            for (stage, pj, pc0, ppp) in pend:
                if stage == 1:
                    s5_stage1(pj, pc0, 128, ppp)
                    nxt.append((2, pj, pc0, ppp))
                else:
                    s5_stage2(pj, pc0, 128, ppp)
            pend.clear()
            pend.extend(nxt)
            if new_item is not None:
                pend.append(new_item)

        def flush_pend():
            while pend:
                advance_pend()

        def s5_unit(j, full, ntok):
            Ct_j = Ct[:, 4 * j:4 * j + 4, :]
            St_j = St[:, 4 * j:4 * j + 4, :]
            rz = rhoZ[:, 4 * j:4 * j + 4, :].rearrange("p a b -> p (a b)")
            js = slice(4 * j, 4 * j + 4)
            flat = lambda t: t.rearrange("p a b -> p (a b)")
            xb = xbT2[j % 2]
            xkey = f"xbT{j % 2}"
            def emit_B(c):
                c0 = c * 128
                fns = []
                for kk in range(4):
                    k = 4 * j + kk
                    fns.append(lambda e, k=k, kk=kk, c0=c0: e.matmul(pbr[:, kk, :], lhsT=Bre[:, k, :], rhs=xb[:, c0:c0 + 128], start=True, stop=True))
                    fns.append(lambda e, k=k, kk=kk, c0=c0: e.matmul(pbi[:, kk, :], lhsT=Bim[:, k, :], rhs=xb[:, c0:c0 + 128], start=True, stop=True))
                S.pe(fns, [xkey, "Bre", "Bim"], ["pb"])

            emit_B(0)
            for c in range(4):
                c0 = c * 128
                pp = c % 2
                dve(lambda e: e.tensor_tensor(out=tA, in0=Ct_j, in1=pbr[:], op=ALU.mult), ["pb", "Ct"], ["tA"])
                dve(lambda e: e.tensor_tensor(out=tB, in0=St_j, in1=pbi[:], op=ALU.mult), ["pb", "St"], ["tB"])
                dve(lambda e: e.tensor_tensor(out=tC, in0=Ct_j, in1=pbi[:], op=ALU.mult), ["pb", "Ct"], ["tC"])
                dve(lambda e: e.tensor_tensor(out=tD, in0=St_j, in1=pbr[:], op=ALU.mult), ["pb", "St"], ["tD"])
                if c < 3:
                    emit_B(c + 1)
                dve(lambda e: e.tensor_tensor(out=sm["p5"][:, 0:4], in0=sm["rho"][:, js], in1=gin2r[:, j, 0:4], op=ALU.mult), ["rho", "gin"], ["p5"])
                dve(lambda e: e.tensor_tensor(out=sm["p6"][:, 0:4], in0=sm["rho"][:, js], in1=gin2i[:, j, 0:4], op=ALU.mult), ["rho", "gin"], ["p6"])
                dve(lambda e: e.tensor_tensor(out=tA, in0=tA, in1=tB, op=ALU.add), ["tA", "tB"], ["tA"])
                dve(lambda e: e.tensor_tensor(out=tC, in0=tC, in1=tD, op=ALU.subtract), ["tC", "tD"], ["tC"])
                dve(lambda e: e.tensor_tensor(out=tA[:, :, 0], in0=tA[:, :, 0], in1=sm["p5"][:, 0:4], op=ALU.add), ["tA", "p5"], ["tA"])
                dve(lambda e: e.tensor_tensor(out=tC[:, :, 0], in0=tC[:, :, 0], in1=sm["p6"][:, 0:4], op=ALU.add), ["tC", "p6"], ["tC"])
                dve(lambda e: e.tensor_tensor_scan(out=flat(gr), data0=rz, data1=flat(tA), initial=0.0, op0=ALU.mult, op1=ALU.add),
                    ["tA", "vraw"], ["gr"])
                dve(lambda e: e.tensor_tensor_scan(out=flat(gi), data0=rz, data1=flat(tC), initial=0.0, op0=ALU.mult, op1=ALU.add),
                    ["tC", "vraw"], ["gi"])
                grl = gr[:, :, 127]; gil = gi[:, :, 127]
                if full:
                    dve(lambda e: e.tensor_tensor(out=P1, in0=Ct_j, in1=gr, op=ALU.mult), ["gr", "Ct"], ["P"])
                    dve(lambda e: e.scalar_tensor_tensor(out=P2, in0=gi, scalar=-1.0, in1=St_j, op0=ALU.mult, op1=ALU.mult), ["gi", "St"], ["P"])
                    dve(lambda e: e.tensor_tensor(out=P3, in0=St_j, in1=gr, op=ALU.mult), ["gr", "St"], ["P"])
                    dve(lambda e: e.tensor_tensor(out=P4, in0=Ct_j, in1=gi, op=ALU.mult), ["gi", "Ct"], ["P"])
                dve(lambda e: e.tensor_tensor(out=sm["p1"][:, 0:4], in0=sm["eTr"][:, js], in1=grl, op=ALU.mult), ["gr", "eTc"], ["p1"])
                dve(lambda e: e.tensor_tensor(out=sm["p2"][:, 0:4], in0=sm["eTi"][:, js], in1=gil, op=ALU.mult), ["gi", "eTs"], ["p2"])
                dve(lambda e: e.tensor_tensor(out=sm["p3"][:, 0:4], in0=sm["eTi"][:, js], in1=grl, op=ALU.mult), ["gr", "eTs"], ["p3"])
                dve(lambda e: e.tensor_tensor(out=sm["p4"][:, 0:4], in0=sm["eTr"][:, js], in1=gil, op=ALU.mult), ["gi", "eTc"], ["p4"])
                dve(lambda e: e.tensor_tensor(out=gin2r[:, j, 0:4], in0=sm["p1"][:, 0:4], in1=sm["p2"][:, 0:4], op=ALU.subtract),
                    ["p1", "p2", "gin"], ["gin"])
                dve(lambda e: e.tensor_tensor(out=gin2i[:, j, 0:4], in0=sm["p3"][:, 0:4], in1=sm["p4"][:, 0:4], op=ALU.add),
                    ["p3", "p4", "gin"], ["gin"])
                if not full:
                    continue
                s5_cproj(j, 128, pp)
                advance_pend((1, j, c0, pp))

        def cmul(out_r, out_i, ar, ai, br, bi, t1, t2, keys_r, key_w):
            dve(lambda e: e.tensor_tensor(out=t1, in0=ar, in1=br, op=ALU.mult), keys_r, [key_w + "t1"])
            dve(lambda e: e.tensor_tensor(out=t2, in0=ai, in1=bi, op=ALU.mult), keys_r, [key_w + "t2"])
            dve(lambda e: e.tensor_tensor(out=out_r, in0=t1, in1=t2, op=ALU.subtract), [key_w + "t1", key_w + "t2"], [key_w + "r"])
            dve(lambda e: e.tensor_tensor(out=t1, in0=ar, in1=bi, op=ALU.mult), keys_r + [key_w + "r"], [key_w + "t1"])
            dve(lambda e: e.tensor_tensor(out=t2, in0=ai, in1=br, op=ALU.mult), keys_r + [key_w + "r"], [key_w + "t2"])
            dve(lambda e: e.tensor_tensor(out=out_i, in0=t1, in1=t2, op=ALU.add), [key_w + "t1", key_w + "t2"], [key_w + "i"])

        def samp_prep():
            S.dma("sp", lambda e: e.dma_start(out=sre_t, in_=sre_d), [], ["sre_t", "vraw"])
            S.dma("sp", lambda e: e.dma_start(out=sim_t, in_=sim_d), [], ["sim_t", "xt", "outt"])
            for src_t, dst, skey in ((sre_t, h0r_v, "sre_t"), (sim_t, h0i_v, "sim_t")):
                for k8 in range(4):
                    S.pe([lambda e, k=k8 * 8 + i, i=i, src_t=src_t: e.transpose(px[:, i * 16:(i + 1) * 16], src_t[0:16, k * 128:(k + 1) * 128],
                                                                                identf[0:16, 0:16]) for i in range(8)],
                         [skey, "identf"], ["px"])
                    dve(lambda e, k8=k8, dst=dst: e.tensor_copy(out=dst[:, k8 * 8:(k8 + 1) * 8, :],
                                                                in_=px[:, 0:128].rearrange("p (a b) -> p a b", a=8)), ["px"], ["h0"])
            cmul(tA_v, tB_v, bc3(sm["icr"], 16), bc3(sm["ici"], 16), h0r_v, h0i_v, tC_v, tD_v, ["h0", "icr", "ici"], "h0p")
            cmul(tC_v, tD_v, bc3(sm["abr"], 16), bc3(sm["abi"], 16), tA_v, tB_v, h0r_v, h0i_v, ["h0pr", "h0pi", "abr", "abi"], "ah")

        def s5_unit_samp(j):
            fns = []
            for kk in range(4):
                k = 4 * j + kk
                fns.append(lambda e, k=k, kk=kk: e.matmul(pbr[:, kk, 0:16], lhsT=Bre[:, k, :], rhs=xbT2[j % 2][:, 0:16], start=True, stop=True))
                fns.append(lambda e, k=k, kk=kk: e.matmul(pbi[:, kk, 0:16], lhsT=Bim[:, k, :], rhs=xbT2[j % 2][:, 0:16], start=True, stop=True))
            S.pe(fns, [f"xbT{j % 2}", "Bre", "Bim"], ["pb"])
            js = slice(4 * j, 4 * j + 4)
            dve(lambda e: e.tensor_tensor(out=tC_v[:, js, :], in0=tC_v[:, js, :], in1=pbr[:, :, 0:16], op=ALU.add), ["pb", "ahr"], ["hp"])
            dve(lambda e: e.tensor_tensor(out=tD_v[:, js, :], in0=tD_v[:, js, :], in1=pbi[:, :, 0:16], op=ALU.add), ["pb", "ahi"], ["hp"])
            dve(lambda e: e.tensor_copy(out=hrb[:, :, 0:16], in_=tC_v[:, js, :]), ["hp"], ["P"])
            dve(lambda e: e.tensor_copy(out=hib[:, :, 0:16], in_=tD_v[:, js, :]), ["hp"], ["P"])
            s5_out(j, 0, 16, hrb, hib)

        def samp_finish():
            cmul(tA_v, tB_v, bc3(sm["cfr"], 16), bc3(sm["cfi"], 16), tC_v, tD_v, h0r_v, h0i_v, ["hp", "cfr", "cfi"], "hs")
            for src_v, dst_t, dram, skey in ((tA_v, sre_t, hsre_o, "hsr"), (tB_v, sim_t, hsim_o, "hsi")):
                for k4 in range(8):
                    S.pe([lambda e, k=k4 * 4 + i, i=i, src_v=src_v: e.transpose(px[0:16, i * 128:(i + 1) * 128], src_v[:, k, :], identf)
                          for i in range(4)], [skey, "identf"], ["px"])
                    dve(lambda e, k4=k4, dst_t=dst_t: e.tensor_copy(out=dst_t[0:16, k4 * 512:(k4 + 1) * 512], in_=px[0:16, :]),
                        ["px"], ["hstok" + skey, "vraw", "xt", "outt"])
                S.dma("sp", lambda e, dst_t=dst_t, dram=dram: e.dma_start(out=dram, in_=dst_t), ["hstok" + skey, "vraw", "xt", "outt"], [], is_out=True)

        dve(lambda e: e.tensor_tensor(out=sm["p1"], in0=sm["dt"], in1=sm["are"], op=ALU.mult), [], ["p1"])
        dve(lambda e: e.tensor_scalar(out=tauR, in0=tauf, scalar1=-1.0, scalar2=127.0, op0=ALU.mult, op1=ALU.add), [], ["tauR"])
        dve(lambda e: e.tensor_tensor(out=targ, in0=bc3(sm["p1"], 128), in1=tauR.unsqueeze(1).to_broadcast([128, 32, 128]), op=ALU.mult),
            ["p1", "tauR"], ["targ"])
        act(lambda e: e.activation(out=targ, in_=targ, func=AF.Exp), ["targ"], ["targ"])
        dve(lambda e: e.tensor_tensor(out=RCt, in0=targ, in1=Ct, op=ALU.mult), ["targ"], ["RCt"])
        dve(lambda e: e.tensor_tensor(out=RSt, in0=targ, in1=St, op=ALU.mult), ["targ"], ["RSt"])
        act(lambda e: e.activation(out=sm["p2"], in_=sm["p1"], func=AF.Exp, scale=128.0), ["p1"], ["p2"])
        dve(lambda e: e.tensor_tensor(out=A128r, in0=sm["p2"], in1=sm["eTr"], op=ALU.mult), ["p2"], ["A128r"])
        dve(lambda e: e.tensor_tensor(out=A128i, in0=sm["p2"], in1=sm["eTi"], op=ALU.mult), ["p2"], ["A128i"])
        dve(lambda e: e.memset(sm["ginr"], 0.0), [], ["H"])
        dve(lambda e: e.memset(sm["gini"], 0.0), ["H"], ["H"])
        S.barrier_all()

        def prev_unit(j):
            xb = xbT2[j % 2]
            xkey = f"xbT{j % 2}"

            def emit_B(c):
                c0 = c * 128
                fns = []
                for kk in range(4):
                    k = 4 * j + kk
                    fns.append(lambda e, k=k, kk=kk, c0=c0: e.matmul(pbr[:, kk, :], lhsT=Bre[:, k, :], rhs=xb[:, c0:c0 + 128], start=True, stop=True))
                    fns.append(lambda e, k=k, kk=kk, c0=c0: e.matmul(pbi[:, kk, :], lhsT=Bim[:, k, :], rhs=xb[:, c0:c0 + 128], start=True, stop=True))
                S.pe(fns, [xkey, "Bre", "Bim"], ["pb"])

            emit_B(0)
            for c in range(4):
                for kk in range(4):
                    k = 4 * j + kk
                    for comp, (tbl, src, sgn) in enumerate(((RCt, pbr, 1.0), (RSt, pbi, 1.0), (RCt, pbi, 1.0), (RSt, pbr, 1.0))):
                        dve(lambda e, k=k, kk=kk, c=c, comp=comp, tbl=tbl, src=src, sgn=sgn: e.scalar_tensor_tensor(
                            out=junkR, in0=tbl[:, k, :], scalar=sgn, in1=src[:, kk, :], op0=ALU.mult, op1=ALU.mult,
                            accum_out=accs[:, c, k, comp:comp + 1]),
                            ["pb", "RCt", "RSt"] + (["accs"] if (j == 0 and c == 0 and kk == 0 and comp == 0) else []),
                            (["accs"] if (j == 7 and c == 3 and kk == 3 and comp == 3) else []))
                if c < 3:
                    emit_B(c + 1)

        def prev_combine():
            H_r, H_i = sm["ginr"], sm["gini"]
            rot_r, rot_i = Ct[:, :, 127], St[:, :, 127]
            for c in range(4):
                dve(lambda e, c=c: e.tensor_tensor(out=sm["p1"], in0=accs[:, c, :, 0], in1=accs[:, c, :, 1], op=ALU.add), ["accs"], ["p1"])
                dve(lambda e, c=c: e.tensor_tensor(out=sm["p2"], in0=accs[:, c, :, 2], in1=accs[:, c, :, 3], op=ALU.subtract), ["accs"], ["p2"])
                cmul(sm["p3"], sm["p4"], rot_r, rot_i, sm["p1"], sm["p2"], sm["p5"], sm["p6"], ["p1", "p2"], "rE")
                cmul(sm["p1"], sm["p2"], A128r, A128i, H_r, H_i, sm["p5"], sm["p6"], ["H", "rEr", "rEi"], "aH")
                dve(lambda e: e.tensor_tensor(out=H_r, in0=sm["p1"], in1=sm["p3"], op=ALU.add), ["aHr", "rEr", "H"], ["H"])
                dve(lambda e: e.tensor_tensor(out=H_i, in0=sm["p2"], in1=sm["p4"], op=ALU.add), ["aHi", "rEi", "H"], ["H"])

        run_pass("prev", xp, 0, None)
        run_pass("prev", xp, 512, None)
        cmul(sm["p1"], sm["p2"], sm["e1r"], sm["e1i"], sm["ginr"], sm["gini"], sm["p3"], sm["p4"], ["H"], "g0")
        dve(lambda e: e.tensor_copy(out=gin2r[:, :, 0:4], in_=sm["p1"].rearrange("p (a b) -> p a b", a=8)), ["g0r"], ["gin"])
        dve(lambda e: e.tensor_copy(out=gin2i[:, :, 0:4], in_=sm["p2"].rearrange("p (a b) -> p a b", a=8)), ["g0i"], ["gin"])
        dve(lambda e: e.tensor_scalar(out=gin2r, in0=gin2r, scalar1=maskv[:, 0:1], scalar2=None, op0=ALU.mult),
            ["gin", "maskv"], ["gin"])
        dve(lambda e: e.tensor_scalar(out=gin2i, in0=gin2i, scalar1=maskv[:, 0:1], scalar2=None, op0=ALU.mult),
            ["gin", "maskv"], ["gin"])
        run_pass("own", xo, 0, yo)
        run_pass("own", xo, 512, yo)
        dve(lambda e: e.tensor_scalar(out=sm["p5"], in0=sm["e1i"], scalar1=-1.0, scalar2=None, op0=ALU.mult), ["e1s"], ["p5"])
        dve(lambda e: e.tensor_copy(out=sm["ginr"].rearrange("p (a b) -> p a b", a=8), in_=gin2r[:, :, 0:4]), ["gin"], ["ginc"])
        dve(lambda e: e.tensor_copy(out=sm["gini"].rearrange("p (a b) -> p a b", a=8), in_=gin2i[:, :, 0:4]), ["gin"], ["ginc"])
        cmul(sm["p1"], sm["p2"], sm["e1r"], sm["p5"], sm["ginr"], sm["gini"], sm["p3"], sm["p4"], ["ginc", "e1c", "p5"], "hf")
        cmul(sm["p5"], sm["p6"], sm["cfr"], sm["cfi"], sm["p1"], sm["p2"], sm["p3"], sm["p4"], ["hfr", "hfi", "cfr", "cfi"], "hF")
        S.pe([lambda e: e.transpose(px[0:32, 0:128], sm["p5"], identf),
              lambda e: e.transpose(px[0:32, 128:256], sm["p6"], identf)], ["hFr", "hFi", "identf"], ["px"])
        dve(lambda e: e.tensor_copy(out=zt[0:32, 0:256], in_=px[0:32, 0:256]), ["px"], ["zt"])
        S.dma("sp", lambda e: e.dma_start(out=hpre_o, in_=zt[0:32, 0:128]), ["zt"], [], is_out=True)
        S.dma("sp", lambda e: e.dma_start(out=hpim_o, in_=zt[0:32, 128:256]), ["zt"], [], is_out=True)
        S.barrier_all()
        run_pass("samp", xs, 0, ys)
        S.finish()
        S.emit(block)
    return nc


_CACHE = {}


def _prep_shared(w_c, b_c, g_pre, w_in, ln_v_g, ln_v_b, w_s, b_s, a_re, a_im, log_dt,
                 b_re, b_im, c_re, c_im, d_skip, w_glu, b_glu, w_out, g_post):
    f = lambda a: np.ascontiguousarray(np.asarray(a, dtype=np.float32))
    sh = {}
    sh["w_c"] = f(w_c); sh["b_c"] = f(b_c).reshape(1, -1)
    sh["gpre_fm"] = f(np.asarray(g_pre).reshape(16, 128).T)
    sh["gpost_fm"] = f(np.asarray(g_post).reshape(16, 128).T)
    sh["w_in"] = f(w_in); sh["w_out"] = f(w_out)
    sh["lng"] = f(ln_v_g).reshape(1, -1); sh["lnb"] = f(ln_v_b).reshape(1, -1)
    ws = np.asarray(w_s, dtype=np.float32)
    sh["ws_t"] = f(ws.transpose(2, 0, 1).reshape(128, 1024))
    sh["bsrow"] = f(b_s).reshape(1, 1024)
    sh["w00"] = f(ws[:, 0, 0]).reshape(1, 8)
    sh["bs0"] = f(np.asarray(b_s)[:, 0]).reshape(1, 8)
    r2 = lambda a: f(np.asarray(a, dtype=np.float32).reshape(32, 2, 64).transpose(1, 2, 0).reshape(128, 32))
    sh["a_re2"] = r2(a_re); sh["a_im2"] = r2(a_im)
    sh["ldt2"] = r2(np.repeat(np.asarray(log_dt, dtype=np.float32)[:, None], 64, axis=1))
    bre = np.asarray(b_re, dtype=np.float32); bim = np.asarray(b_im, dtype=np.float32)
    cre = np.asarray(c_re, dtype=np.float32); cim = np.asarray(c_im, dtype=np.float32)
    Bre_pad = np.zeros((128, 32, 128), np.float32); Bim_pad = np.zeros((128, 32, 128), np.float32)
    Cre_pad = np.zeros((128, 32, 128), np.float32); Cim_pad = np.zeros((128, 32, 128), np.float32)
    for k in range(32):
        for q in range(2):
            g = 2 * k + q
            gl = 2 * (k % 4) + q
            Bre_pad[gl * 16:(gl + 1) * 16, k, q * 64:(q + 1) * 64] = bre[g].T
            Bim_pad[gl * 16:(gl + 1) * 16, k, q * 64:(q + 1) * 64] = bim[g].T
            Cre_pad[q * 64:(q + 1) * 64, k, gl * 16:(gl + 1) * 16] = cre[g].T
            Cim_pad[q * 64:(q + 1) * 64, k, gl * 16:(gl + 1) * 16] = cim[g].T
    sh["Bre_pad"] = Bre_pad.reshape(128, 4096); sh["Bim_pad"] = Bim_pad.reshape(128, 4096)
    sh["Cre_pad"] = Cre_pad.reshape(128, 4096); sh["Cim_pad"] = Cim_pad.reshape(128, 4096)
    sh["dfm"] = f(np.asarray(d_skip).reshape(8, 128).T)
    wg = np.asarray(w_glu, dtype=np.float32); bg = np.asarray(b_glu, dtype=np.float32)
    wgv = np.zeros((128, 8, 128), np.float32); wgg = np.zeros((128, 8, 128), np.float32)
    bgv = np.zeros((128, 8), np.float32); bgg = np.zeros((128, 8), np.float32)
    for j in range(8):
        for gl in range(8):
            g = 8 * j + gl
            wgv[gl * 16:(gl + 1) * 16, j, gl * 16:(gl + 1) * 16] = wg[g, :, 0:16]
            wgg[gl * 16:(gl + 1) * 16, j, gl * 16:(gl + 1) * 16] = wg[g, :, 16:32]
            bgv[gl * 16:(gl + 1) * 16, j] = bg[g, 0:16]
            bgg[gl * 16:(gl + 1) * 16, j] = bg[g, 16:32]
    sh["wgv_pad"] = wgv.reshape(128, 1024); sh["wgg_pad"] = wgg.reshape(128, 1024)
    sh["bgv"] = bgv; sh["bgg"] = bgg
    return sh


def kernel(x_prompt, x_sample, c_prompt, c_sample, state_b_re, state_b_im,
           w_c, b_c, g_pre, w_in, ln_v_g, ln_v_b, w_s, b_s, a_re, a_im, log_dt,
           b_re, b_im, c_re, c_im, d_skip, w_glu, b_glu, w_out, g_post):
    f = lambda a: np.ascontiguousarray(np.asarray(a, dtype=np.float32))
    x_prompt = f(x_prompt); x_sample = f(x_sample); c_prompt = f(c_prompt); c_sample = f(c_sample)
    state_b_re = f(state_b_re); state_b_im = f(state_b_im)
    sh = _prep_shared(w_c, b_c, g_pre, w_in, ln_v_g, ln_v_b, w_s, b_s, a_re, a_im, log_dt,
                      b_re, b_im, c_re, c_im, d_skip, w_glu, b_glu, w_out, g_post)
    in_maps = []
    for i in range(NCORES):
        b, hf = i // 2, i % 2
        m = dict(sh)
        m["xo"] = f(x_prompt[b, hf * 1024:(hf + 1) * 1024])
        m["xp"] = f(x_prompt[b, 0:1024])
        m["xs"] = f(x_sample[i * 16:(i + 1) * 16, 0])
        cs = np.zeros((33, D), np.float32)
        cs[0:16] = c_sample[i * 16:(i + 1) * 16]
        cs[32] = c_prompt[b]
        m["cs"] = cs
        m["sre"] = f(state_b_re[i * 16:(i + 1) * 16].reshape(16, 4096))
        m["sim"] = f(state_b_im[i * 16:(i + 1) * 16].reshape(16, 4096))
        m["maskv"] = np.full((128, 1), float(hf), np.float32)
        in_maps.append(m)
    if "nc" not in _CACHE:
        _CACHE["nc"] = build_nc()
    res = run_bass_kernel_spmd(_CACHE["nc"], in_maps, core_ids=list(range(NCORES)))
    R = res.results
    y_prompt = np.zeros((4, 2048, D), np.float32)
    y_sample = np.zeros((128, 1, D), np.float32)
    v_s = np.zeros((128, 1, 1024), np.float32)
    hp_re = np.zeros((4, 64, 64), np.float32); hp_im = np.zeros((4, 64, 64), np.float32)
    hs_re = np.zeros((128, 64, 64), np.float32); hs_im = np.zeros((128, 64, 64), np.float32)
    unr2 = lambda a: np.asarray(a).reshape(64, 64)
    for i in range(NCORES):
        b, hf = i // 2, i % 2
        y_prompt[b, hf * 1024:(hf + 1) * 1024] = R[i]["yo"]
        y_sample[i * 16:(i + 1) * 16, 0] = R[i]["ys"]
        v_s[i * 16:(i + 1) * 16, 0] = R[i]["vs"]
        hs_re[i * 16:(i + 1) * 16] = np.asarray(R[i]["hs_re"]).reshape(16, 64, 64)
        hs_im[i * 16:(i + 1) * 16] = np.asarray(R[i]["hs_im"]).reshape(16, 64, 64)
        if hf == 1:
            hp_re[b] = unr2(R[i]["hp_re"]); hp_im[b] = unr2(R[i]["hp_im"])
    return (y_prompt, y_sample, v_s, hp_re, hp_im, hs_re, hs_im)
```
